# Optimizing a Trainium2 kernel written in Bass

```python
import math
import jax, jax.numpy as jnp
from jax import lax
import numpy as np


D_MODEL = 1024
BATCH = 8
SEQ = 4096
DEPTH = 1

CHUNK = 64
Q_BLOCK = 128
MIX_WIDTH = D_MODEL
DA_WIDTH = MIX_WIDTH // 2
DA_HEADS = 4
DA_V_DIM = DA_WIDTH // DA_HEADS
DA_QK_DIM = DA_V_DIM // 2
RET_WIDTH = MIX_WIDTH - DA_WIDTH
RET_HEADS = 4
RET_V_DIM = RET_WIDTH // RET_HEADS
RET_QK_DIM = RET_V_DIM // 2
SPLIT_SIZES = (DA_HEADS * 2 * DA_QK_DIM, DA_HEADS * 2 * DA_QK_DIM, DA_HEADS * DA_V_DIM,
               RET_HEADS * RET_QK_DIM, RET_HEADS * RET_QK_DIM, RET_HEADS * RET_V_DIM,
               RET_HEADS * RET_V_DIM)
IN_WIDTH = sum(SPLIT_SIZES)
D_FF = ((8 * D_MODEL // 3 + 127) // 128) * 128
CONV_WIDTH = 3
N_MOD = 6
RMS_EPS = 1e-6

kernel_name = 'hybrid_diffattn_retention_convffn_block'


def rms_norm(x, gain=None):
    xf = x.astype(jnp.float32)
    y = xf * lax.rsqrt(jnp.mean(xf * xf, axis=-1, keepdims=True) + RMS_EPS)
    if gain is not None:
        y = y * gain.astype(jnp.float32)
    return y.astype(x.dtype)


def split_cols(p, sizes):
    idx = []
    acc = 0
    for s in sizes[:-1]:
        acc += s
        idx.append(acc)
    return jnp.split(p, idx, axis=-1)


def diff_attention(q, k, v, lam, subln_gain, lambda_init):
    B, S, H, _, dq = q.shape
    dv = v.shape[-1]
    n_qb = S // Q_BLOCK
    scale = dq ** -0.5
    slopes = 2.0 ** (-8.0 * (jnp.arange(H, dtype=jnp.float32) + 1.0) / H)
    k_pos = jnp.arange(S)
    k_chunk = k_pos // CHUNK
    q_blocks = q.reshape(B, n_qb, Q_BLOCK, H, 2, dq).transpose(1, 0, 2, 3, 4, 5)

    def block(args):
        qb, start = args
        q_pos = start + jnp.arange(Q_BLOCK)
        s = jnp.einsum('bqhmd,bkhmd->bhmqk', qb, k).astype(jnp.float32) * scale
        dist = jnp.abs(q_pos[:, None] - k_pos[None, :]).astype(jnp.float32)
        bias = -slopes[:, None, None] * dist[None]
        allowed = k_chunk[None, :] <= (q_pos // CHUNK)[:, None]
        s = jnp.where(allowed, s + bias[None, :, None], -jnp.inf)
        p = jax.nn.softmax(s, axis=-1)
        a = p[:, :, 0] - lam * p[:, :, 1]
        return jnp.einsum('bhqk,bkhe->bqhe', a.astype(v.dtype), v)

    starts = jnp.arange(n_qb, dtype=jnp.int32) * Q_BLOCK
    o = lax.map(block, (q_blocks, starts))
    o = o.transpose(1, 0, 2, 3, 4).reshape(B, S, H, dv)
    o = rms_norm(o, subln_gain) * (1.0 - lambda_init)
    return o.reshape(B, S, H * dv)


def retention(q, k, v, g):
    B, S, H, dk = q.shape
    dv = v.shape[-1]
    N = S // CHUNK
    dt = q.dtype
    log_gamma = jnp.log(1.0 - 2.0 ** (-5.0 - jnp.arange(H, dtype=jnp.float32)))
    idx = jnp.arange(CHUNK, dtype=jnp.float32)
    intra_decay = jnp.exp(log_gamma[:, None, None] * jnp.abs(idx[:, None] - idx[None, :]))
    in_decay = jnp.exp(log_gamma[:, None] * (CHUNK - 1.0 - idx)).T
    out_decay = jnp.exp(log_gamma[:, None] * (idx + 1.0)).T
    chunk_decay = jnp.exp(log_gamma * CHUNK)

    qc = q.reshape(B, N, CHUNK, H, dk)
    kc = k.reshape(B, N, CHUNK, H, dk) * (dk ** -0.5)
    vc = v.reshape(B, N, CHUNK, H, dv)

    s = jnp.einsum('bnihd,bnjhd->bnhij', qc, kc) * intra_decay.astype(dt)
    o_intra = jnp.einsum('bnhij,bnjhe->bnihe', s, vc)

    u = jnp.einsum('bnjhd,bnjhe->nbhde', kc * in_decay[:, :, None].astype(dt), vc).astype(jnp.float32)

    def step(state, u_n):
        return chunk_decay[None, :, None, None] * state + u_n, state

    _, s_prev = lax.scan(step, jnp.zeros((B, H, dk, dv), jnp.float32), u)
    o_cross = jnp.einsum('bnihd,nbhde->bnihe', qc * out_decay[:, :, None].astype(dt), s_prev.astype(dt))

    o = (o_intra + o_cross).reshape(B, S, H, dv)
    o = rms_norm(o).reshape(B, S, H * dv)
    return o * jax.nn.silu(g)


def causal_dwconv(h, w, b):
    K = w.shape[0]
    S = h.shape[1]
    hp = jnp.pad(h, ((0, 0), (K - 1, 0), (0, 0)))
    out = b
    for j in range(K):
        out = out + hp[:, j:j + S] * w[j]
    return out


def setup_inputs(seed: int = 0) -> dict:
    key = jax.random.key(seed)
    ks = jax.random.split(key, 20)

    def nrm(k, shape, scale):
        return jax.random.normal(k, shape, jnp.float32) * scale

    def gain(k, n):
        return 1.0 + nrm(k, (DEPTH, n), 0.02)

    return {
        'x': nrm(ks[0], (BATCH, SEQ, D_MODEL), 1.0),
        'c': nrm(ks[1], (BATCH, D_MODEL), 1.0),
        'w_ada': nrm(ks[2], (DEPTH, D_MODEL, N_MOD * D_MODEL), 0.5 * D_MODEL ** -0.5),
        'b_ada': nrm(ks[3], (DEPTH, N_MOD * D_MODEL), 0.01),
        'g_pre_mix': gain(ks[4], D_MODEL),
        'w_in': nrm(ks[5], (DEPTH, D_MODEL, IN_WIDTH), D_MODEL ** -0.5),
        'lam_q1': nrm(ks[6], (DEPTH, DA_QK_DIM), 0.1),
        'lam_k1': nrm(ks[7], (DEPTH, DA_QK_DIM), 0.1),
        'lam_q2': nrm(ks[8], (DEPTH, DA_QK_DIM), 0.1),
        'lam_k2': nrm(ks[9], (DEPTH, DA_QK_DIM), 0.1),
        'g_da_subln': gain(ks[10], DA_V_DIM),
        'w_out': nrm(ks[11], (DEPTH, MIX_WIDTH, D_MODEL), MIX_WIDTH ** -0.5),
        'g_post_mix': gain(ks[12], D_MODEL),
        'g_pre_ffn': gain(ks[13], D_MODEL),
        'w_up': nrm(ks[14], (DEPTH, D_MODEL, 2 * D_FF), D_MODEL ** -0.5),
        'conv_w': nrm(ks[15], (DEPTH, CONV_WIDTH, 2 * D_FF), CONV_WIDTH ** -0.5),
        'conv_b': nrm(ks[16], (DEPTH, 2 * D_FF), 0.01),
        'w_down': nrm(ks[17], (DEPTH, D_FF, D_MODEL), D_FF ** -0.5),
        'g_post_ffn': gain(ks[18], D_MODEL),
    }


def reference(x, c, w_ada, b_ada, g_pre_mix, w_in, lam_q1, lam_k1, lam_q2, lam_k2, g_da_subln,
              w_out, g_post_mix, g_pre_ffn, w_up, conv_w, conv_b, w_down, g_post_ffn):
    B, S, _ = x.shape
    for l in range(DEPTH):
        mod = jax.nn.silu(c) @ w_ada[l] + b_ada[l]
        sh1, sc1, gt1, sh2, sc2, gt2 = [m[:, None, :] for m in jnp.split(mod, N_MOD, axis=-1)]

        h = rms_norm(x, g_pre_mix[l]) * (1.0 + sc1) + sh1
        proj = h @ w_in[l]
        da_q, da_k, da_v, r_q, r_k, r_v, r_g = split_cols(proj, SPLIT_SIZES)

        lambda_init = 0.8 - 0.6 * math.exp(-0.3 * l)
        lam = (jnp.exp(jnp.sum(lam_q1[l].astype(jnp.float32) * lam_k1[l].astype(jnp.float32)))
               - jnp.exp(jnp.sum(lam_q2[l].astype(jnp.float32) * lam_k2[l].astype(jnp.float32)))
               + lambda_init)
        o_da = diff_attention(da_q.reshape(B, S, DA_HEADS, 2, DA_QK_DIM),
                              da_k.reshape(B, S, DA_HEADS, 2, DA_QK_DIM),
                              da_v.reshape(B, S, DA_HEADS, DA_V_DIM),
                              lam, g_da_subln[l], lambda_init)
        o_ret = retention(r_q.reshape(B, S, RET_HEADS, RET_QK_DIM),
                          r_k.reshape(B, S, RET_HEADS, RET_QK_DIM),
                          r_v.reshape(B, S, RET_HEADS, RET_V_DIM),
                          r_g)
        mix = jnp.concatenate([o_da, o_ret], axis=-1) @ w_out[l]
        x = x + gt1 * rms_norm(mix, g_post_mix[l])

        h = rms_norm(x, g_pre_ffn[l]) * (1.0 + sc2) + sh2
        u = causal_dwconv(h @ w_up[l], conv_w[l], conv_b[l])
        u_gate, u_val = jnp.split(u, 2, axis=-1)
        f = (jax.nn.silu(u_gate) * u_val) @ w_down[l]
        x = x + gt2 * rms_norm(f, g_post_ffn[l])
    return x
```

```python
import math
import numpy as np
import concourse.bass as bass
import concourse.mybir as mybir
from concourse.bass_utils import run_bass_kernel_spmd

F32 = mybir.dt.float32
BF16 = mybir.dt.bfloat16
ALU = mybir.AluOpType
AF = mybir.ActivationFunctionType
AX = mybir.AxisListType

S_LEN = 4096
D = 1024
INW = 3072
DFF = 2816
NFC = 22
NEG = -1.0e6


class Buf:
    def __init__(self, name):
        self.name = name
        self.w = None
        self.r = []
        self.al = []
        self.dsem = None
        self.dcnt = 0


def alias(a, b):
    if b not in a.al:
        a.al.append(b)
    if a not in b.al:
        b.al.append(a)


class EngQ:
    def __init__(self, name, sem):
        self.name = name
        self.sem = sem
        self.n = 0
        self.ops = []
        self.waited = {}


class Sched:
    ENGS = ("pe", "act", "dve", "pool", "sp")

    def __init__(self, nc, stack):
        self.nc = nc
        self.stack = stack
        self.q = {}
        for e in self.ENGS:
            sem = stack.enter_context(nc.semaphore("s_" + e))
            self.q[e] = EngQ(e, sem)

    def _deps(self, reads, writes):
        raw, other = [], []
        for b in reads:
            for bb in [b] + b.al:
                if bb.w is not None:
                    raw.append(bb.w)
        for b in writes:
            for bb in [b] + b.al:
                if bb.w is not None:
                    other.append(bb.w)
                other.extend(bb.r)
        return raw, other

    def _waits(self, q, raw, other):
        need = {}
        for (sem, val) in raw:
            if sem is q.sem and q.name == "pe":
                continue
            if q.waited.get(sem, 0) < val:
                need[sem] = max(need.get(sem, 0), val)
        for (sem, val) in other:
            if sem is q.sem and q.name == "pe":
                continue
            if q.waited.get(sem, 0) < val:
                need[sem] = max(need.get(sem, 0), val)
        for sem, val in need.items():
            q.waited[sem] = val
        return list(need.items())

    def op(self, eng, fn, reads=(), writes=()):
        q = self.q[eng]
        raw, other = self._deps(reads, writes)
        waits = self._waits(q, raw, other)
        q.n += 1
        tk = (q.sem, q.n)
        q.ops.append((waits, fn, (q.sem, 1)))
        for b in reads:
            b.r.append(tk)
            if len(b.r) > 64:
                b.r = self._compact(b.r)
        for b in writes:
            b.w = tk
            b.r = []
        return tk

    @staticmethod
    def _compact(lst):
        best = {}
        for sem, val in lst:
            k = id(sem)
            if k not in best or best[k][1] < val:
                best[k] = (sem, val)
        return list(best.values())

    def dma(self, eng, out, in_, reads=(), writes=(), track=None, **kw):
        q = self.q[eng]
        raw, other = self._deps(reads, writes)
        waits = self._waits(q, raw + other, [])
        tb = track or (writes[0] if writes else reads[0])
        if tb.dsem is None:
            tb.dsem = self.stack.enter_context(self.nc.semaphore("d_" + tb.name))
        tb.dcnt += 16
        tk = (tb.dsem, tb.dcnt)

        def fn(e, out=out, in_=in_, kw=kw):
            return e.dma_start(out=out, in_=in_, **kw)

        q.ops.append((waits, fn, (tb.dsem, 16)))
        for b in reads:
            b.r.append(tk)
        for b in writes:
            b.w = tk
            b.r = []
        return tk

    def final_wait(self, eng, bufs):
        q = self.q[eng]
        deps = []
        for b in bufs:
            if b.w is not None:
                deps.append(b.w)
            deps.extend(b.r)
        waits = self._waits(q, deps, [])
        q.ops.append((waits, None, None))

    def emit(self):
        nc = self.nc
        with nc.Block() as block:
            def run(q):
                def body(e):
                    for waits, fn, inc in q.ops:
                        for sem, val in waits:
                            e.wait_ge(sem, val)
                        if fn is not None:
                            fn(e).then_inc(inc[0], inc[1])
                return body
            block.tensor(run(self.q["pe"]))
            block.scalar(run(self.q["act"]))
            block.vector(run(self.q["dve"]))
            block.gpsimd(run(self.q["pool"]))
            block.sync(run(self.q["sp"]))


class T:
    def __init__(self, h, name, lo=0, hi=0):
        self.h = h
        self.b = Buf(name)
        self.lo = lo
        self.hi = hi

    def __getitem__(self, k):
        return self.h[k]


class Alloc:
    BASE = 16512
    END = 229376

    def __init__(self, nc):
        self.nc = nc
        self.cur = self.BASE
        self.cnt = 0

    @staticmethod
    def nbytes(shape, dt):
        n = 1
        for s in shape[1:]:
            n *= s
        return n * (4 if dt == F32 else 2)

    def at(self, name, shape, dt, off):
        self.cnt += 1
        h = self.nc.alloc_sbuf_tensor_at("%s_%d" % (name, self.cnt), shape, dt, offset=off)
        return T(h, name, off, off + self.nbytes(shape, dt))

    def new(self, name, shape, dt):
        off = (self.cur + 31) // 32 * 32
        t = self.at(name, shape, dt, off)
        self.cur = t.hi
        assert self.cur <= self.END, ("SBUF overflow", name, self.cur)
        return t


class Arena:
    def __init__(self, alloc, size):
        self.alloc = alloc
        self.lo = (alloc.cur + 31) // 32 * 32
        self.hi = self.lo + size
        alloc.cur = self.hi
        assert alloc.cur <= alloc.END, ("SBUF overflow arena", alloc.cur)
        self.sets = {}
        self.curs = {}

    def new(self, setname, name, shape, dt):
        cur = self.curs.get(setname, self.lo)
        off = (cur + 31) // 32 * 32
        t = self.alloc.at(name, shape, dt, off)
        assert t.hi <= self.hi, ("arena overflow", setname, name, t.hi - self.lo)
        self.curs[setname] = t.hi
        for sn, lst in self.sets.items():
            if sn == setname:
                continue
            for o in lst:
                if o.lo < t.hi and t.lo < o.hi:
                    alias(o.b, t.b)
        self.sets.setdefault(setname, []).append(t)
        return t


def build(NSB=8, dbg=None, stop_after=None):
    from contextlib import ExitStack
    nc = bass.Bass("TRN2", target_bir_lowering=False)

    def din(name, shape):
        return nc.dram_tensor(name, shape, F32, kind="ExternalInput").ap()

    x = din("x", [S_LEN, D])
    rowpack = din("rowpack", [1, 5504])
    w_ada = din("w_ada", [D, 6 * D])
    b_ada = din("b_ada", [1, 6 * D])
    w_in = din("w_in", [D, INW])
    w_out = din("w_out", [D, D])
    w_up = din("w_up", [D, 2 * DFF])
    conv_w = din("conv_w", [3, 2 * DFF])
    conv_b = din("conv_b", [1, 2 * DFF])
    w_down = din("w_down", [DFF, D])
    y = nc.dram_tensor("y", [S_LEN, D], F32, kind="ExternalOutput").ap()
    dbg_out = {}
    if dbg:
        for k, shp in dbg.items():
            dbg_out[k] = nc.dram_tensor("dbg_" + k, shp, F32, kind="ExternalOutput").ap()

    s_win = nc.dram_tensor("s_win", [D, INW], BF16).ap()
    s_wout = nc.dram_tensor("s_wout", [D, D], BF16).ap()
    s_wup = nc.dram_tensor("s_wup", [D, 11, 512], BF16).ap()
    s_wdn = nc.dram_tensor("s_wdn", [DFF, D], BF16).ap()
    B_swin = [Buf("swin%d" % g) for g in range(6)]
    B_swout = Buf("swout")
    B_swup = [Buf("swup%d" % g) for g in range(4)]
    B_swdn = [Buf("swdn%d" % g) for g in range(2)]

    with ExitStack() as st:
        S = Sched(nc, st)
        A = Alloc(nc)

        def psum(name, shape, dt):
            return T(st.enter_context(nc.psum_tensor(name, shape, dt)), name)

        PB = [psum("pb%d" % i, [128, 512], F32) for i in range(7)]
        PT = psum("pT", [128, 1024], BF16)
        PTf = T(PT.h.bitcast(F32), "pTf")
        PTf.b = PT.b
        gen_i = [0]
        att_i = [0]

        def abank():
            b = (PB[0], PB[1], PTf)[att_i[0] % 3]
            att_i[0] += 1
            return b

        def gbank():
            b = PB[gen_i[0] % 4]
            gen_i[0] += 1
            return b

        ffn_i = [0]

        def fbank():
            b = PB[ffn_i[0] % 7]
            ffn_i[0] += 1
            return b

        KT = A.new("KT", [128, 4, S_LEN], BF16)
        KTb = [[Buf("KT%d_%d" % (h, s)) for s in range(8)] for h in range(4)]
        VA = A.new("VA", [128, 32, 4, 130], BF16)
        VAb = [Buf("VA%d" % t) for t in range(32)]
        ident = A.new("ident", [128, 128], BF16)
        Dt = A.new("Dt", [128, 1024], F32)
        CB = A.new("CB", [128, 4, 32], F32)
        VS = A.new("VS", [128, 4, 128], F32)
        CB3 = A.new("CB3", [128, 4, 4], F32)
        Dt2 = A.new("Dt2", [128, 1024], F32)
        IDEC = A.new("IDEC", [128, 4, 64], F32)
        ODEC = A.new("ODEC", [128, 2, 64], F32)
        INDEC = A.new("INDEC", [128, 4, 64], F32)
        G64 = A.new("G64", [128, 2], F32)
        G1 = A.new("G1", [128, D], F32)
        G2 = A.new("G2", [128, D], F32)
        GS = A.new("GS", [128, 128], F32)
        colA = A.new("colA", [128, 16], F32)
        colB = A.new("colB", [128, 16], F32)
        convc = A.new("convc", [128, 4, 2 * NFC], F32)
        HALO = A.new("HALO", [128, 2, 2 * NFC, 2], F32)
        HALOb = [[Buf("halo%d_%d" % (p_, c_)) for c_ in range(2 * NFC)] for p_ in range(2)]
        Sst = A.new("Sst", [128, 2, 128], F32)
        Sbd = A.new("Sbd", [128, 2, 128], BF16)
        neglam = A.new("neglam", [128, 1], F32)
        ones1 = A.new("ones1", [1, 128], F32)
        epsc = A.new("epsc", [128, 1], F32)
        sccol = A.new("sccol", [128, 8], BF16)
        stat = A.new("stat", [128, 64], F32)
        statb = [Buf("stat%d" % i) for i in range(64)]
        ln8 = A.new("ln8", [128, 1], F32)
        ln05 = A.new("ln05", [128, 1], F32)
        idx63 = A.new("idx63", [128, 1], F32)
        lam_t = A.new("lam_t", [1, 8], F32)
        WS = [A.new("WS%d" % i, [128, 8, 512], BF16) for i in range(2)]
        hT = A.new("hT", [128, 8, 512], BF16)
        XS = [A.new("XS%d" % i, [128, D], F32) for i in range(2)]
        XN = [A.new("XN%d" % i, [128, D], BF16) for i in range(2)]
        junk = A.new("junk", [128, D], BF16)
        X1 = [A.new("X1_%d" % i, [128, D], F32) for i in range(4)]
        AR = Arena(A, A.END - ((A.cur + 31) // 32 * 32))
        arena_size = AR.hi - AR.lo
        print("arena size", arena_size, "persistent+work", AR.lo - A.BASE)

        qTbd = AR.new("M", "qTbd", [128, 4, 2, 512], BF16)
        rqTbd = AR.new("M", "rqTbd", [128, 2, 8, 2, 64], BF16)
        rkT = AR.new("M", "rkT", [128, 2, 512], BF16)
        rkdbd = [AR.new("M", "rkdbd%d" % t, [128, 2, 256], BF16) for t in range(4)]
        rv = [AR.new("M", "rv%d" % t, [128, 512], BF16) for t in range(4)]
        sg = [AR.new("M", "sg%d" % t, [128, 512], BF16) for t in range(4)]
        sTbd = [AR.new("M", "sTbd%d" % i, [128, 2, 256], BF16) for i in range(2)]
        tT = [AR.new("M", "tT%d" % i, [128, 512], F32) for i in range(2)]
        Pm = [AR.new("M", "Pm%d" % i, [128, 512], BF16) for i in range(4)]
        oda = [AR.new("M", "oda%d" % i, [128, 4, 128], F32) for i in range(2)]
        sig = AR.new("M", "sig", [128, 512], F32)
        araw = AR.new("M", "araw", [128, 1056], F32)
        mix = [AR.new("M", "mix%d" % t, [128, D], BF16) for t in range(4)]
        mixT = AR.new("M", "mixT", [128, 8, 512], BF16)
        actT = AR.new("F", "actT", [128, NFC, 512], BF16)
        WD = [AR.new("F", "WD%d" % i, [128, NFC, 256], BF16) for i in range(2)]
        Ug = [AR.new("F", "Ug%d" % i, [128, 514], F32) for i in range(2)]
        Uv = [AR.new("F", "Uv%d" % i, [128, 514], F32) for i in range(2)]
        Yg = [AR.new("F", "Yg%d" % i, [128, 512], F32) for i in range(2)]
        Yv = [AR.new("F", "Yv%d" % i, [128, 512], F32) for i in range(2)]
        UH = {id(u_): Buf("uh_" + u_.b.name) for u_ in Ug + Uv}
        fsb = [AR.alloc.at("fsb%d" % t, [128, D], F32, Ug[0].lo + t * 4096) for t in range(4)]
        for u_ in Ug + Uv:
            for o in list(u_.b.al):
                alias(UH[id(u_)], o)
        for ft in fsb:
            assert ft.hi <= AR.hi
            for o in Ug + Uv + Yg + Yv:
                if o.lo < ft.hi and ft.lo < o.hi:
                    alias(o.b, ft.b)
                    if id(o) in UH:
                        alias(UH[id(o)], ft.b)
            for o in AR.sets["M"]:
                if o.lo < ft.hi and ft.lo < o.hi:
                    alias(o.b, ft.b)
        print("set sizes", {k: v - AR.lo for k, v in AR.curs.items()})
        rowp = AR.new("P", "rowp", [1, 5504], F32)
        wast = [AR.new("P", "wast%d" % i, [128, 8, 256], BF16) for i in range(3)]
        bada = [AR.new("P", "bada%d" % i, [1, 1024], F32) for i in range(2)]
        modp = [AR.new("P", "modp%d" % i, [1, 1024], F32) for i in range(2)]
        crow = [AR.new("P", "crow%d" % i, [1, 1408], F32) for i in range(1)]
        tmpr = AR.new("P", "tmpr", [1, 1024], F32)
        scrow = AR.new("P", "scrow", [1, 1024], F32)

        actTb = [Buf("actT%d" % c_) for c_ in range(NFC)]
        for b_ in actTb:
            for o in list(actT.b.al):
                alias(b_, o)
        mixTb = [Buf("mixT%d" % t_) for t_ in range(4)]
        for b_ in mixTb:
            for o in list(mixT.b.al):
                alias(b_, o)

        def OP(eng, fn, r=(), w=()):
            return S.op(eng, fn, [t.b if isinstance(t, T) else t for t in r],
                        [t.b if isinstance(t, T) else t for t in w])

        def DMA(eng, out, in_, r=(), w=(), **kw):
            return S.dma(eng, out, in_, [t.b if isinstance(t, T) else t for t in r],
                         [t.b if isinstance(t, T) else t for t in w], **kw)

        def mm(out, lhsT, rhs, start, stop, r, w, skip=False):
            if skip:
                return OP("pe", lambda e: e.matmul(out, lhsT=lhsT, rhs=rhs, start=start, stop=stop,
                                                   skip_group_check=True), r, w)
            return OP("pe", lambda e: e.matmul(out, lhsT=lhsT, rhs=rhs, start=start, stop=stop), r, w)

        def act(out, in_, func, r, w, **kw):
            return OP("act", lambda e: e.activation(out, in_, func, **kw), r, w)

        def ts(eng, out, in0, s1, s2, op0, op1, r, w):
            if s2 is None:
                return OP(eng, lambda e: e.tensor_scalar(out, in0, s1, None, op0), r, w)
            return OP(eng, lambda e: e.tensor_scalar(out, in0, s1, s2, op0, op1), r, w)

        def stt(eng, out, in0, sc, in1, op0, op1, r, w):
            return OP(eng, lambda e: e.scalar_tensor_tensor(out, in0, sc, in1, op0, op1), r, w)

        def tt(eng, out, in0, in1, op, r, w):
            return OP(eng, lambda e: e.tensor_tensor(out, in0, in1, op), r, w)

        def cp(eng, out, in_, r, w):
            if eng == "act":
                return act(out, in_, AF.Copy, r, w)
            return OP(eng, lambda e: e.tensor_copy(out, in_), r, w)

        def recip(out, in_, r, w):
            return OP("dve", lambda e: e.reciprocal(out, in_), r, w)

        def memset(eng, ap, val, w):
            return OP(eng, lambda e: e.memset(ap, val), (), w)

        def rstd_from(dst, src, n, bufs_r, bufs_w):
            act(dst, src, AF.Ln, list(bufs_r) + [epsc], bufs_w, scale=1.0 / n, bias=epsc[:])
            act(dst, dst, AF.Exp, bufs_w, bufs_w, scale=-0.5)

        def cast_win(g):
            DMA("pool", s_win[:, g * 512:(g + 1) * 512], w_in[:, g * 512:(g + 1) * 512], w=[B_swin[g]])

        def early_casts():
            for g in range(1, 6):
                cast_win(g)
            for hf in range(2):
                DMA("pool", s_wout[:, hf * 512:(hf + 1) * 512], w_out[:, hf * 512:(hf + 1) * 512], w=[B_swout])

        cast_win(0)
        def deferred_casts():
            for g in range(11):
                bb = B_swup[min(g // 3, 3)]
                DMA("pool", s_wup[:, g, 0:256], w_up[:, g * 256:(g + 1) * 256], w=[bb])
                DMA("pool", s_wup[:, g, 256:512], w_up[:, DFF + g * 256:DFF + (g + 1) * 256], w=[bb])
            for qq in range(4):
                DMA("pool", s_wdn[:, qq * 256:(qq + 1) * 256], w_down[:, qq * 256:(qq + 1) * 256],
                    w=[B_swdn[qq // 2]])

        DMA("sp", rowp[:], rowpack, w=[rowp])
        memset("pool", ones1[:], 1.0, [ones1])
        memset("pool", epsc[:], 1e-6, [epsc])
        memset("pool", ln05[:], math.log(0.5), [ln05])
        idf = tT[0]
        OP("pool", lambda e: e.iota(idf[:, 0:128], [[1, 128]], base=0, channel_multiplier=-1,
                                    allow_small_or_imprecise_dtypes=True), (), [idf])
        OP("pool", lambda e: e.tensor_single_scalar(ident[:], idf[:, 0:128], 0.0, ALU.is_equal), [idf], [ident])
        OP("pool", lambda e: e.iota(Dt[:], [[1, 1024]], base=-384, channel_multiplier=-1,
                                    allow_small_or_imprecise_dtypes=True), (), [Dt])
        act(Dt[:], Dt[:], AF.Abs, [Dt], [Dt])
        ts("dve", Dt[:], Dt[:], -1.0, None, ALU.mult, None, [Dt], [Dt])
        memset("pool", Dt[64:128, 384:448], NEG, [Dt])
        slopes = [2.0 ** (-8.0 * (h + 1) / 4) for h in range(4)]
        for h in range(4):
            OP("pool", lambda e, h=h: e.iota(CB[:, h, :], [[1, 32]], base=-1, channel_multiplier=0,
                                             allow_small_or_imprecise_dtypes=True), (), [CB])
        for h in range(4):
            ts("dve", CB[:, h, :], CB[:, h, :], -slopes[h] * 128.0, None, ALU.mult, None, [CB], [CB])
        for h in range(4):
            OP("pool", lambda e, h=h: e.iota(CB3[:, h, :], [[128, 4]], base=0, channel_multiplier=0,
                                             allow_small_or_imprecise_dtypes=True), (), [CB3])
        for h in range(4):
            ts("dve", CB3[:, h, :], CB3[:, h, :], slopes[h], None, ALU.mult, None, [CB3], [CB3])
        OP("pool", lambda e: e.iota(VS[:, 1, :], [[0, 128]], base=0, channel_multiplier=1,
                                    allow_small_or_imprecise_dtypes=True), (), [VS])
        for h in (3, 2, 1):
            act(VS[:, h, :], VS[:, 1, :], AF.Exp, [VS], [VS], scale=slopes[h])
        memset("pool", VS[:, 0, :], 1.0, [VS])
        OP("pool", lambda e: e.iota(Dt2[:], [[1, 1024]], base=-384, channel_multiplier=-1,
                                    allow_small_or_imprecise_dtypes=True), (), [Dt2])
        tt("dve", Dt2[:], Dt2[:], Dt[:], ALU.add, [Dt2, Dt], [Dt2])
        lg = [math.log(1.0 - 2.0 ** (-5.0 - h)) for h in range(4)]
        memset("pool", ln8[:], math.log(0.125), [ln8])
        for half in range(2):
            ps_ = slice(half * 64, half * 64 + 64)
            OP("pool", lambda e, ps_=ps_: e.iota(IDEC[ps_, 0, :], [[1, 64]], base=0, channel_multiplier=-1,
                                                 allow_small_or_imprecise_dtypes=True), (), [IDEC])
            OP("pool", lambda e, ps_=ps_: e.iota(INDEC[ps_, 0, 0:1], [[1, 1]], base=63, channel_multiplier=-1,
                                                 allow_small_or_imprecise_dtypes=True), (), [INDEC])
        act(IDEC[:, 0, :], IDEC[:, 0, :], AF.Abs, [IDEC], [IDEC])
        OP("pool", lambda e: e.iota(IDEC[:, 1, :], [[1, 64]], base=1, channel_multiplier=0,
                                    allow_small_or_imprecise_dtypes=True), (), [IDEC])
        tt("dve", IDEC[:, 3, :], IDEC[:, 0, :], IDEC[:, 1, :], ALU.subtract, [IDEC], [IDEC])
        for h in (0, 1, 2, 3):
            act(IDEC[:, h, :], IDEC[:, 3, :], AF.Exp, [IDEC, ln8], [IDEC], scale=lg[h], bias=ln8[:])
        cp("dve", idx63[:], INDEC[:, 0, 0:1], [INDEC], [idx63])
        for h in range(4):
            act(INDEC[:, h, :], idx63[:].to_broadcast([128, 64]), AF.Exp, [idx63, ln8], [INDEC],
                scale=lg[h], bias=ln8[:])
        OP("pool", lambda e: e.iota(ODEC[:, 0, :], [[1, 64]], base=1, channel_multiplier=0,
                                    allow_small_or_imprecise_dtypes=True), (), [ODEC])
        for a_ in (1, 0):
            for hp in range(2):
                ps_ = slice(hp * 64, hp * 64 + 64)
                act(ODEC[ps_, a_, :], ODEC[ps_, 0, :], AF.Exp, [ODEC], [ODEC], scale=lg[2 * a_ + hp])
        for a_ in range(2):
            for hp in range(2):
                memset("pool", G64[hp * 64:(hp + 1) * 64, a_:a_ + 1], math.exp(lg[2 * a_ + hp] * 64.0), [G64])
        memset("pool", Sst[:], 0.0, [Sst])
        memset("pool", Sbd[:], 0.0, [Sbd])
        memset("pool", HALO[:], 0.0, [b_ for l_ in HALOb for b_ in l_])
        memset("pool", VA[:], 1.0, VAb)
        for h in range(1, 4):
            ts("dve", VA[:, :, h, 128:129], VA[:, :, h, 128:129], VS[:, h, 0:1], None, ALU.mult, None,
               VAb + [VS], VAb)

        crow_c = rowp[0:1, 0:1024]
        act(tmpr[:], crow_c, AF.Exp, [rowp], [tmpr], scale=-1.0)
        ts("dve", tmpr[:], tmpr[:], 1.0, None, ALU.add, None, [tmpr], [tmpr])
        recip(tmpr[:], tmpr[:], [tmpr], [tmpr])
        tt("dve", scrow[:], tmpr[:], crow_c, ALU.mult, [tmpr, rowp], [scrow])

        def row_to_cols(row_ap_fn, n, dst, dst_col0, rbufs, wbuf):
            pb = gbank()
            for j in range(n):
                mm(pb[:, j:j + 1], row_ap_fn(j), ones1[0:1, 0:1], True, True, list(rbufs) + [ones1], [pb])
            cp("dve", dst[:, dst_col0:dst_col0 + n], pb[:, 0:n], [pb], [wbuf])

        row_to_cols(lambda j: scrow[0:1, j * 128:(j + 1) * 128], 8, sccol, 0, [scrow], sccol)

        wi = [0]
        for cb in range(2):
            bpiece = bada[cb % 2]
            mpiece = modp[cb % 2]
            DMA("sp", bpiece[:], b_ada[0:1, cb * 1024:(cb + 1) * 1024], w=[bpiece])
            for sub in range(4):
                c0 = cb * 1024 + sub * 256
                wsb = wast[wi[0] % 3]
                wi[0] += 1
                DMA("pool", wsb[:], w_ada[:, c0:c0 + 256].rearrange("(k p) n -> p k n", p=128), w=[wsb])
                pb = gbank()
                for k in range(8):
                    mm(pb[0:1, 0:256], sccol[:, k:k + 1], wsb[:, k, :], k == 0, k == 7, [sccol, wsb], [pb])
                tt("dve", mpiece[0:1, sub * 256:(sub + 1) * 256], pb[0:1, 0:256],
                   bpiece[0:1, sub * 256:(sub + 1) * 256], ALU.add, [pb, bpiece], [mpiece])
            gcol = {1: 1024, 2: 2048, 4: 3072, 5: 4096}
            if cb in (0, 3):
                dst = colA if cb == 0 else colB
                row_to_cols(lambda j, mpiece=mpiece: mpiece[0:1, j * 128:(j + 1) * 128], 8, dst, 8, [mpiece], dst)
            elif cb in (1, 4):
                dst = colA if cb == 1 else colB
                stt("dve", tmpr[:], mpiece[:], 1.0, rowp[0:1, gcol[cb]:gcol[cb] + 1024], ALU.add, ALU.mult,
                    [mpiece, rowp], [tmpr])
                row_to_cols(lambda j: tmpr[0:1, j * 128:(j + 1) * 128], 8, dst, 0, [tmpr], dst)
            else:
                dst = G1 if cb == 2 else G2
                tt("dve", tmpr[:], mpiece[:], rowp[0:1, gcol[cb]:gcol[cb] + 1024], ALU.mult, [mpiece, rowp], [tmpr])
                for hf in range(2):
                    pb = gbank()
                    mm(pb[:], ones1[0:1, :], tmpr[0:1, hf * 512:(hf + 1) * 512], True, True, [ones1, tmpr], [pb])
                    cp("act", dst[:, hf * 512:(hf + 1) * 512], pb[:], [pb], [dst])

        early_casts()
        lq = 5120
        for i in range(2):
            tt("dve", tmpr[0:1, 0:64], rowp[0:1, lq + i * 128:lq + i * 128 + 64],
               rowp[0:1, lq + i * 128 + 64:lq + i * 128 + 128], ALU.mult, [rowp], [tmpr])
            OP("dve", lambda e, i=i: e.reduce_sum(lam_t[0:1, i:i + 1], tmpr[0:1, 0:64], axis=AX.X), [tmpr], [lam_t])
        act(lam_t[0:1, 2:4], lam_t[0:1, 0:2], AF.Exp, [lam_t], [lam_t])
        tt("dve", lam_t[0:1, 4:5], lam_t[0:1, 3:4], lam_t[0:1, 2:3], ALU.subtract, [lam_t], [lam_t])
        ts("dve", lam_t[0:1, 5:6], lam_t[0:1, 4:5], -0.2, None, ALU.add, None, [lam_t], [lam_t])
        pb = gbank()
        mm(pb[:, 0:1], ones1[0:1, :], lam_t[0:1, 5:6], True, True, [ones1, lam_t], [pb])
        cp("dve", neglam[:], pb[:, 0:1], [pb], [neglam])
        pb = gbank()
        mm(pb[:, 0:128], ones1[0:1, :], rowp[0:1, 5376:5504], True, True, [ones1, rowp], [pb])
        act(GS[:], pb[:, 0:128], AF.Copy, [pb], [GS], scale=0.8)
        dwast = []
        for i in range(3):
            t_ = A.at("dwast%d" % i, [128, 8, 256], BF16, X1[i].lo)
            alias(t_.b, X1[i].b)
            dwast.append(t_)
        drow = []
        for i in range(4):
            t_ = A.at("drow%d" % i, [1, 256], F32, X1[3].lo + i * 1024)
            alias(t_.b, X1[3].b)
            drow.append(t_)

        def mod_gen():
            goff = {2: 2048, 4: 3072, 5: 4096}
            for p_ in range(16):
                cb = 2 + p_ // 4
                sub = p_ % 4
                c0 = cb * 1024 + sub * 256
                wsb = dwast[p_ % 3]
                DMA("pool", wsb[:], w_ada[:, c0:c0 + 256].rearrange("(k p) n -> p k n", p=128), w=[wsb])
                brow, grow, mrow = drow[0], drow[1], drow[2 + p_ % 2]
                DMA("sp", brow[:], b_ada[0:1, c0:c0 + 256], w=[brow])
                if cb in goff:
                    DMA("sp", grow[:], rowpack[0:1, goff[cb] + sub * 256:goff[cb] + (sub + 1) * 256], w=[grow])
                yield
                pb = gbank()
                for k in range(8):
                    mm(pb[0:1, 0:256], sccol[:, k:k + 1], wsb[:, k, :], k == 0, k == 7, [sccol, wsb], [pb])
                tt("dve", mrow[:], pb[0:1, 0:256], brow[:], ALU.add, [pb, brow], [mrow])
                if cb == 3:
                    row_to_cols(lambda j, mrow=mrow: mrow[0:1, j * 128:(j + 1) * 128], 2, colB, 8 + 2 * sub, [mrow], colB)
                elif cb == 4:
                    stt("dve", mrow[:], mrow[:], 1.0, grow[:], ALU.add, ALU.mult, [mrow, grow], [mrow])
                    row_to_cols(lambda j, mrow=mrow: mrow[0:1, j * 128:(j + 1) * 128], 2, colB, 2 * sub, [mrow], colB)
                else:
                    dst = G1 if cb == 2 else G2
                    tt("dve", mrow[:], mrow[:], grow[:], ALU.mult, [mrow, grow], [mrow])
                    pb2 = gbank()
                    mm(pb2[:, 0:256], ones1[0:1, :], mrow[0:1, :], True, True, [ones1, mrow], [pb2])
                    cp("act", dst[:, sub * 256:(sub + 1) * 256], pb2[:, 0:256], [pb2], [dst])
                yield

        modg = mod_gen()

        dcrow = []
        for i in range(2):
            t_ = A.at("dcrow%d" % i, [1, 1408], F32, X1[2 * i].lo)
            alias(t_.b, X1[2 * i].b)
            alias(t_.b, X1[2 * i + 1].b)
            for o in dwast + drow:
                if o.lo < t_.hi and t_.lo < o.hi:
                    alias(t_.b, o.b)
            dcrow.append(t_)

        def conv_cols():
            ci_ = 0
            for r_ in range(4):
                for pc in range(4):
                    cr = dcrow[ci_ % 2]
                    ci_ += 1
                    src = (conv_w[r_:r_ + 1, pc * 1408:(pc + 1) * 1408] if r_ < 3
                           else conv_b[0:1, pc * 1408:(pc + 1) * 1408])
                    DMA("sp", cr[:], src, w=[cr])
                    pb = gbank()
                    for j in range(11):
                        mm(pb[:, j:j + 1], cr[0:1, j * 128:(j + 1) * 128], ones1[0:1, 0:1], True, True,
                           [cr, ones1], [pb])
                    cp("dve", convc[:, r_, pc * 11:(pc + 1) * 11], pb[:, 0:11], [pb], [convc])


        pend_epi = []
        pref_w = {}
        ws_i = [0]
        xs_i = [0]
        xn_i = [0]

        def load_w(src_ap, rbuf):
            wsb = WS[ws_i[0] % 2]
            ws_i[0] += 1
            DMA("sp", wsb[:], src_ap, r=[rbuf], w=[wsb])
            return wsb

        def norm_tile(t, src, sb_, col):
            sq = statb[t]
            act(junk[:], src, AF.Square, [sb_], [junk, sq], accum_out=stat[:, t:t + 1])
            rstd_from(stat[:, 4 + t:5 + t], stat[:, t:t + 1], D, [sq], [statb[4 + t]])
            xnb = XN[xn_i[0] % 2]
            xn_i[0] += 1
            act(xnb[:], src, AF.Copy, [sb_, statb[4 + t]], [xnb], scale=stat[:, 4 + t:5 + t])
            for k in range(8):
                OP("pe", lambda e, k=k, xnb=xnb: e.transpose(PT[:, k * 128:(k + 1) * 128],
                                                              xnb[:, k * 128:(k + 1) * 128], ident[:]),
                   [xnb, ident], [PT])
            for k in range(8):
                ts("dve", hT[:, k, t * 128:(t + 1) * 128], PT[:, k * 128:(k + 1) * 128],
                   col[:, k:k + 1], col[:, 8 + k:9 + k], ALU.mult, ALU.add, [PT, col], [hT])

        def rezero():
            memset("pool", qTbd[:], 0.0, [qTbd])
            memset("pool", rqTbd[:], 0.0, [rqTbd])
            for t in range(4):
                memset("pool", rkdbd[t][:], 0.0, [rkdbd[t]])
            for i in range(2):
                memset("pool", sTbd[i][:], 0.0, [sTbd[i]])

        def phaseA_tile(sbn, t):
            xb = XS[xs_i[0] % 2]
            xs_i[0] += 1
            DMA("sp", xb[:], x[sbn * 512 + t * 128:sbn * 512 + (t + 1) * 128, :], w=[xb])
            norm_tile(t, xb[:], xb, colA)

        def phaseA(sbn):
            for t in range(4):
                phaseA_tile(sbn, t)

        class _Stop(Exception):
            pass

        def chk(tag):
            if stop_after == tag:
                raise _Stop()

        def _main():
          for sb in range(NSB):
              t0 = sb * 512
              if sb == 0:
                  phaseA(0)

              if stop_after == 'A':
                  break
              if sb == 0:
                  rezero()

              ev = [0]

              def ev_eng():
                  ev[0] += 1
                  return "act" if ev[0] % 2 else "dve"

              def fm_chunk(wsb, jc, consume):
                  if sb == 0:
                      next(modg, None)
                  pb = gbank()
                  for k in range(8):
                      mm(pb[:], wsb[:, k, jc * 128:(jc + 1) * 128], hT[:, k, :], k == 0, k == 7, [wsb, hT], [pb])
                  consume(pb)

              def tm_tile(wsb, t, c0, n, consume):
                  if sb == 0:
                      next(modg, None)
                  pb = gbank()
                  for k in range(8):
                      mm(pb[:, 0:n], hT[:, k, t * 128:(t + 1) * 128], wsb[:, k, c0:c0 + n], k == 0, k == 7,
                         [wsb, hT], [pb])
                  consume(pb)

              def wgrp(g):
                  if g >= 1 and pend_epi:
                      pend_epi.pop(0)()
                  if g in pref_w:
                      return pref_w.pop(g)
                  return load_w(s_win[:, g * 512:(g + 1) * 512].rearrange("(k p) n -> p k n", p=128), B_swin[g])

              wsb = wgrp(0)
              for h in range(4):
                  def cons(pb, h=h):
                      cp("act", qTbd[0:64, h, 0, :], pb[0:64, :], [pb], [qTbd])
                      cp("dve", qTbd[64:128, h, 1, :], pb[64:128, :], [pb], [qTbd])
                  fm_chunk(wsb, h, cons)
              chk('B0')
              wsb = wgrp(1)
              for h in range(4):
                  def cons(pb, h=h):
                      cp(ev_eng(), KT[:, h, t0:t0 + 512], pb[:], [pb], [KTb[h][sb]])
                  fm_chunk(wsb, h, cons)
              chk('B1')
              wsb = wgrp(2)
              for t in range(4):
                  def cons(pb, t=t):
                      tt("dve", VA[:, sb * 4 + t, :, 0:128], pb[:].rearrange("p (h e) -> p h e", e=128), VS[:],
                         ALU.mult, [pb, VS], [VAb[sb * 4 + t]])
                  tm_tile(wsb, t, 0, 512, cons)
              chk('B2')
              wsb = wgrp(3)
              for a_ in range(2):
                  def cons(pb, a_=a_):
                      for hp in range(2):
                          ps_ = slice(hp * 64, hp * 64 + 64)
                          for n in range(8):
                              tt("dve", rqTbd[ps_, a_, n, hp, :], pb[ps_, n * 64:(n + 1) * 64], ODEC[ps_, a_, :],
                                 ALU.mult, [pb, ODEC], [rqTbd])
                  fm_chunk(wsb, a_, cons)
              for a_ in range(2):
                  def cons(pb, a_=a_):
                      cp(ev_eng(), rkT[:, a_, :], pb[:], [pb], [rkT])
                  fm_chunk(wsb, 2 + a_, cons)
              for t in range(4):
                  def cons(pb, t=t):
                      ind = INDEC[:].rearrange("p h d -> p (h d)")
                      tt("dve", rkdbd[t][0:64, 0, :], pb[0:64, 0:256], ind[0:64, :], ALU.mult, [pb, INDEC], [rkdbd[t]])
                      tt("dve", rkdbd[t][64:128, 1, :], pb[64:128, 0:256], ind[64:128, :], ALU.mult,
                         [pb, INDEC], [rkdbd[t]])
                  tm_tile(wsb, t, 256, 256, cons)
              chk('B3')
              wsb = wgrp(4)
              for t in range(4):
                  def cons(pb, t=t):
                      cp(ev_eng(), rv[t][:], pb[:], [pb], [rv[t]])
                  tm_tile(wsb, t, 0, 512, cons)
              chk('B4')
              wsb = wgrp(5)
              for t in range(4):
                  def cons(pb, t=t):
                      act(sig[:], pb[:], AF.Tanh, [pb], [sig], scale=0.5)
                      stt("dve", sg[t][:], sig[:], 1.0, pb[:], ALU.add, ALU.mult, [sig, pb], [sg[t]])
                  tm_tile(wsb, t, 0, 512, cons)

              if stop_after == 'B':
                  break
              if sb == 0:
                  for _ in modg:
                      pass
                  deferred_casts()
              def ret_gen():
                  rb = statb[60]
                  idec = IDEC[:].rearrange("p h i -> p (h i)")
                  for t in range(4):
                      psS = PB[2]
                      for c_ in range(2):
                          n = 2 * t + c_
                          for a_ in range(2):
                              mm(psS[c_ * 64:(c_ + 1) * 64, a_ * 128:(a_ + 1) * 128].rearrange("p (b i) -> p b i", b=2),
                                 rkT[:, a_, n * 64:(n + 1) * 64], rqTbd[:, a_, n, :, :], True, True, [rkT, rqTbd], [psS])
                      yield
                      stb = sTbd[t % 2]
                      for c_ in range(2):
                          tt("dve", stb[c_ * 64:(c_ + 1) * 64, c_, :], psS[c_ * 64:(c_ + 1) * 64, 0:256],
                             idec[c_ * 64:(c_ + 1) * 64, :], ALU.mult, [psS, IDEC], [stb])
                      yield
                      psO = PB[3]
                      for c_ in range(2):
                          for h in range(4):
                              mm(psO[c_ * 64:(c_ + 1) * 64, h * 128:(h + 1) * 128], stb[:, c_, h * 64:(h + 1) * 64],
                                 rv[t][:, h * 128:(h + 1) * 128], h == 0, False, [stb, rv[t]], [psO], skip=True)
                      for c_ in range(2):
                          n = 2 * t + c_
                          for h in range(4):
                              mm(psO[c_ * 64:(c_ + 1) * 64, h * 128:(h + 1) * 128],
                                 rqTbd[:, h // 2, n, h % 2, :], Sbd[:, h // 2, :], False, False, [rqTbd, Sbd], [psO],
                                 skip=True)
                          psU = PB[2]
                          for h in range(4):
                              mm(psU[(h % 2) * 64:(h % 2) * 64 + 64, (h // 2) * 128:(h // 2) * 128 + 128],
                                 rkdbd[t][:, c_, h * 64:(h + 1) * 64], rv[t][:, h * 128:(h + 1) * 128], True, True,
                                 [rkdbd[t], rv[t]], [psU])
                          yield
                          for a_ in range(2):
                              stt("dve", Sst[:, a_, :], Sst[:, a_, :], G64[:, a_:a_ + 1], psU[:, a_ * 128:(a_ + 1) * 128],
                                  ALU.mult, ALU.add, [Sst, G64, psU], [Sst])
                          cp("dve", Sbd[:], Sst[:], [Sst], [Sbd])
                          yield
                      for h in range(4):
                          act(junk[:, 0:128], psO[:, h * 128:(h + 1) * 128], AF.Square, [psO], [junk, rb],
                              accum_out=stat[:, 56 + h:57 + h])
                      act(stat[:, 56:60], stat[:, 56:60], AF.Ln, [rb, epsc], [rb], scale=1.0 / 128, bias=epsc[:])
                      act(stat[:, 56:60], stat[:, 56:60], AF.Exp, [rb, ln05], [rb], scale=-0.5, bias=ln05[:])
                      yield
                      for h in range(4):
                          stt("dve", mix[t][:, 512 + h * 128:512 + (h + 1) * 128], psO[:, h * 128:(h + 1) * 128],
                              stat[:, 56 + h:57 + h], sg[t][:, h * 128:(h + 1) * 128], ALU.mult, ALU.mult,
                              [psO, rb, sg[t]], [mix[t]])
                      yield

              retg = ret_gen()
              n_items_total = 4 * 6 + 2 * sum(min(4 * sb, w_) for w_ in (4, 16, 64, 256))
              ret_stride = max(1, n_items_total // 30)
              ret_cnt = [0]
              nkb = 4 * sb + 4
              tt_i = [0]
              pm_i = [0]
              od_i = [0]
              pend_fin = []
              WINDOW = [4, 16, 64, 256]
              for h in range(4):
                  cscale = slopes[h] * 8.0
                  items = []
                  for kb in range(nkb):
                      i = kb - 4 * sb
                      if i < 0:
                          tbk = 4 * sb - kb
                          if tbk > WINDOW[h]:
                              continue
                          for qa in (0, 256):
                              items.append((kb, qa, qa + 256, qa + 512, tbk))
                      else:
                          lo = 128 * i
                          ranges = []
                          if lo < 256:
                              ranges.append((lo, 256))
                              ranges.append((256, 512))
                          else:
                              ranges.append((lo, 512))
                          for qa, qb in ranges:
                              items.append((kb, qa, qb, qa - 128 * i + 384, None))
                  accX, accY, accZ = PB[4], PB[5], PB[6]

                  def region(m, qt):
                      if qt < 3:
                          bank = accX if m == 0 else accY
                          return bank, qt * 132
                      return accZ, m * 132

                  started = set()
                  sbanks = {}

                  def do_qk(it):
                      kb, qa, qb, dc0, tbk = it
                      w_ = qb - qa
                      pb = abank()
                      mm(pb[:, 0:2 * w_].rearrange("p (m q) -> p m q", m=2),
                         KT[:, h, kb * 128:(kb + 1) * 128], qTbd[:, h, :, qa:qb], True, True,
                         [KTb[h][kb // 4], qTbd], [pb])
                      sbanks[it] = pb

                  def do_rest(it):
                      kb, qa, qb, dc0, tbk = it
                      w_ = qb - qa
                      pb = sbanks.pop(it)
                      pm_ = Pm[pm_i[0] % 4]
                      pm_i[0] += 1
                      if h == 0:
                          tb_ = tT[tt_i[0] % 2]
                          tt_i[0] += 1
                          for m in range(2):
                              stt("dve", tb_[:, m * w_:(m + 1) * w_], Dt[:, dc0:dc0 + w_], cscale,
                                  pb[:, m * w_:(m + 1) * w_], ALU.mult, ALU.add, [Dt, pb], [tb_])
                          if tbk is None:
                              act(pm_[:, 0:2 * w_], tb_[:, 0:2 * w_], AF.Exp, [tb_], [pm_], scale=0.125)
                          else:
                              act(pm_[:, 0:2 * w_], tb_[:, 0:2 * w_], AF.Exp, [tb_, CB], [pm_], scale=0.125,
                                  bias=CB[:, h, tbk:tbk + 1])
                      elif tbk is None:
                          ii = kb - 4 * sb
                          tb_ = tT[tt_i[0] % 2]
                          tt_i[0] += 1
                          for m in range(2):
                              stt("dve", tb_[:, m * w_:(m + 1) * w_], Dt2[:, dc0:dc0 + w_], cscale,
                                  pb[:, m * w_:(m + 1) * w_], ALU.mult, ALU.add, [Dt2, pb], [tb_])
                          act(pm_[:, 0:2 * w_], tb_[:, 0:2 * w_], AF.Exp, [tb_, CB3], [pm_], scale=0.125,
                              bias=CB3[:, h, ii:ii + 1])
                      else:
                          act(pm_[:, 0:2 * w_], pb[:, 0:2 * w_], AF.Exp, [pb], [pm_], scale=0.125,
                              bias=float(-slopes[h] * 128.0 * tbk))
                      for m in range(2):
                          for qt in range(qa // 128, qb // 128):
                              bank, c0 = region(m, qt)
                              first = id(bank) not in started
                              started.add(id(bank))
                              off = m * w_ + (qt * 128 - qa)
                              mm(bank[:, c0:c0 + 129], pm_[:, off:off + 128], VA[:, kb, h, 0:129],
                                 first, False, [pm_, VAb[kb]], [bank], skip=True)

                  for idx, it in enumerate(items):
                      if idx == 0:
                          do_qk(it)
                          if len(items) > 1:
                              do_qk(items[1])
                      if idx + 2 < len(items):
                          do_qk(items[idx + 2])
                      do_rest(it)
                      if pend_fin and (idx == 2 or idx == len(items) - 1):
                          pend_fin.pop(0)()
                      ret_cnt[0] += 1
                      if ret_cnt[0] % ret_stride == 0:
                          next(retg, None)

                  v3 = lambda ap, n: ap.rearrange("p (r c) -> p r c", c=132)[:, 0:n, 0:129]
                  cp("dve", v3(araw[:, 0:396], 3), v3(accX[:, 0:396], 3), [accX], [araw])
                  cp("dve", v3(araw[:, 396:792], 3), v3(accY[:, 0:396], 3), [accY], [araw])
                  cp("dve", v3(araw[:, 792:1056], 2), v3(accZ[:, 0:264], 2), [accZ], [araw])

                  def finalize(h=h):
                      ob = oda[od_i[0] % 2]
                      od_i[0] += 1
                      sc0 = 8 + h * 12
                      colsb = [statb[8 + h]]

                      def rreg(m, qt):
                          return (m * 396 + qt * 132) if qt < 3 else (792 + m * 132)

                      recip(stat[:, sc0:sc0 + 3], araw[:, 0:396].rearrange("p (r c) -> p r c", c=132)[:, :, 128],
                            [araw], colsb)
                      recip(stat[:, sc0 + 4:sc0 + 7], araw[:, 396:792].rearrange("p (r c) -> p r c", c=132)[:, :, 128],
                            [araw], colsb)
                      recip(stat[:, sc0 + 3:sc0 + 4], araw[:, 920:921], [araw], colsb)
                      recip(stat[:, sc0 + 7:sc0 + 8], araw[:, 1052:1053], [araw], colsb)
                      ts("dve", stat[:, sc0 + 4:sc0 + 8], stat[:, sc0 + 4:sc0 + 8], neglam[:, 0:1], None, ALU.mult, None,
                         colsb + [neglam], colsb)
                      for qt in range(4):
                          c1 = rreg(0, qt)
                          c2 = rreg(1, qt)
                          ts("dve", ob[:, qt, :], araw[:, c1:c1 + 128], stat[:, sc0 + qt:sc0 + qt + 1], None, ALU.mult,
                             None, [araw] + colsb, [ob])
                          stt("dve", ob[:, qt, :], araw[:, c2:c2 + 128], stat[:, sc0 + 4 + qt:sc0 + 5 + qt], ob[:, qt, :],
                              ALU.mult, ALU.add, [araw, ob] + colsb, [ob])
                          act(junk[:, 0:128], ob[:, qt, :], AF.Square, [ob], [junk] + colsb,
                              accum_out=stat[:, sc0 + 8 + qt:sc0 + 9 + qt])
                      rstd_from(stat[:, sc0 + 8:sc0 + 12], stat[:, sc0 + 8:sc0 + 12], 128, colsb, colsb)
                      for qt in range(4):
                          stt("dve", mix[qt][:, h * 128:(h + 1) * 128], ob[:, qt, :], stat[:, sc0 + 8 + qt:sc0 + 9 + qt],
                              GS[:], ALU.mult, ALU.mult, [ob, GS] + colsb, [mix[qt]])

                  pend_fin.append(finalize)
                  if h == 3:
                      while pend_fin:
                          pend_fin.pop(0)()

              if stop_after == 'C':
                  break
              for _ in retg:
                  pass

              if stop_after == 'D':
                  break
              if sb == 0:
                  conv_cols()
              wo = [load_w(s_wout[:, hf * 512:(hf + 1) * 512].rearrange("(k p) n -> p k n", p=128), B_swout)
                    for hf in range(2)]
              mb = statb[61]
              for t in range(4):
                  for k in range(8):
                      OP("pe", lambda e, k=k, t=t: e.transpose(PT[:, k * 128:(k + 1) * 128],
                                                                mix[t][:, k * 128:(k + 1) * 128], ident[:]),
                         [mix[t], ident], [PT])
                  cp("act", mixT[:, :, t * 128:(t + 1) * 128], PT[:].rearrange("p (k q) -> p k q", k=8), [PT], [mixTb[t]])
                  xb = XS[xs_i[0] % 2]
                  xs_i[0] += 1
                  DMA("sp", xb[:], x[t0 + t * 128:t0 + (t + 1) * 128, :], w=[xb])
                  pbs = []
                  for hf in range(2):
                      pb = gbank()
                      for k in range(8):
                          mm(pb[:], mixT[:, k, t * 128:(t + 1) * 128], wo[hf][:, k, :], k == 0, k == 7,
                             [mixTb[t], wo[hf]], [pb])
                      act(junk[:, 0:512], pb[:], AF.Square, [pb], [junk, mb], accum_out=stat[:, 60 + hf:61 + hf])
                      pbs.append(pb)
                  tt("dve", stat[:, 62:63], stat[:, 60:61], stat[:, 61:62], ALU.add, [mb], [mb])
                  rstd_from(stat[:, 62:63], stat[:, 62:63], D, [mb], [mb])
                  for hf in range(2):
                      cs = slice(hf * 512, (hf + 1) * 512)
                      stt("dve", X1[t][:, cs], pbs[hf][:], stat[:, 62:63], G1[:, cs], ALU.mult, ALU.mult,
                          [pbs[hf], mb, G1], [X1[t]])
                      tt("dve", X1[t][:, cs], X1[t][:, cs], xb[:, cs], ALU.add, [X1[t], xb], [X1[t]])
                  if stop_after != 'E' and t >= 2:
                      norm_tile(t - 2, X1[t - 2][:], X1[t - 2], colB)
              if stop_after != 'E':
                  norm_tile(2, X1[2][:], X1[2], colB)
                  norm_tile(3, X1[3][:], X1[3], colB)

              if dbg and "x1" in dbg and sb == 0:
                  for t in range(4):
                      DMA("sp", dbg_out["x1"][t * 128:(t + 1) * 128, :], X1[t][:], r=[X1[t]])

              if stop_after == 'E':
                  break
              wd_i = [0]
              u_i = [0]
              pend_mul = []
              for g in range(11):
                  wsb = load_w(s_wup[:, g, :].rearrange("(k p) n -> p k n", p=128), B_swup[min(g // 3, 3)])
                  for j in range(2):
                      cidx = g * 2 + j
                      ug, uv = Ug[u_i[0] % 2], Uv[u_i[0] % 2]
                      yg, yv = Yg[u_i[0] % 2], Yv[u_i[0] % 2]
                      u_i[0] += 1
                      for (ut, yt, coff, cc) in ((ug, yg, j * 128, cidx), (uv, yv, 256 + j * 128, NFC + cidx)):
                          pb = fbank()
                          for k in range(8):
                              mm(pb[:], wsb[:, k, coff:coff + 128], hT[:, k, :], k == 0, k == 7, [wsb, hT], [pb])
                          hp_, hn_ = sb % 2, (sb + 1) % 2
                          uh = UH[id(ut)]
                          cp("pool", ut[:, 0:2], HALO[:, hp_, cc, :], [HALOb[hp_][cc]], [uh])
                          cp("act", ut[:, 2:514], pb[:], [pb], [ut])
                          act(yt[:], pb[:], AF.Identity, [pb, convc], [yt], scale=convc[:, 2, cc:cc + 1],
                              bias=convc[:, 3, cc:cc + 1])
                          cp("pool", HALO[:, hn_, cc, :], ut[:, 512:514], [ut], [HALOb[hn_][cc]])
                          stt("dve", yt[:], ut[:, 1:513], convc[:, 1, cc:cc + 1], yt[:], ALU.mult, ALU.add,
                              [ut, uh, convc, yt], [yt])
                          stt("dve", yt[:], ut[:, 0:512], convc[:, 0, cc:cc + 1], yt[:], ALU.mult, ALU.add,
                              [ut, uh, convc, yt], [yt])
                      if pend_mul:
                          pend_mul.pop()()
                      act(yg[:], yg[:], AF.Silu, [yg], [yg])
                      pend_mul.append(lambda yg=yg, yv=yv, cidx=cidx: tt("pool", actT[:, cidx, :], yg[:], yv[:], ALU.mult,
                                                                         [yg, yv], [actTb[cidx]]))

              pend_mul.pop()()
              fb = statb[62]
              for qq in range(4):
                  wdb = WD[wd_i[0] % 2]
                  wd_i[0] += 1
                  DMA("sp", wdb[:], s_wdn[:, qq * 256:(qq + 1) * 256].rearrange("(c p) n -> p c n", p=128),
                      r=[B_swdn[qq // 2]], w=[wdb])
                  for t in range(4):
                      pb = gbank()
                      for c_ in range(NFC):
                          mm(pb[:, 0:256], actT[:, c_, t * 128:(t + 1) * 128], wdb[:, c_, :], c_ == 0, c_ == NFC - 1,
                             [actTb[c_], wdb], [pb])
                      cp("dve", fsb[t][:, qq * 256:(qq + 1) * 256], pb[:, 0:256], [pb], [fsb[t]])
                  if sb + 1 < NSB:
                      phaseA_tile(sb + 1, qq)
              if sb + 1 < NSB:
                  rezero()
              if sb + 1 < NSB:
                  for g_ in range(2):
                      pref_w[g_] = load_w(s_win[:, g_ * 512:(g_ + 1) * 512].rearrange("(k p) n -> p k n", p=128),
                                          B_swin[g_])

              def epi_tile(t, t0=t0, fb=fb):
                  act(junk[:], fsb[t][:], AF.Square, [fsb[t]], [junk, fb], accum_out=stat[:, 63:64])
                  rstd_from(stat[:, 63:64], stat[:, 63:64], D, [fb], [fb])
                  stt("dve", fsb[t][:], fsb[t][:], stat[:, 63:64], G2[:], ALU.mult, ALU.mult, [fsb[t], fb, G2], [fsb[t]])
                  tt("dve", X1[t][:], X1[t][:], fsb[t][:], ALU.add, [X1[t], fsb[t]], [X1[t]])
                  DMA("pool", y[t0 + t * 128:t0 + (t + 1) * 128, :], X1[t][:], r=[X1[t]])

              for t in range(4):
                  pend_epi.append(lambda t=t, f=epi_tile: f(t))
              if sb + 1 == NSB:
                  while pend_epi:
                      pend_epi.pop(0)()


        try:
            _main()
        except _Stop:
            pass
        S.final_wait("sp", [t.b for t in X1])
        S.emit()
    return nc


_CACHE = {}


def _inputs_for_core(b, inp):
    f = lambda a: np.ascontiguousarray(np.asarray(a, dtype=np.float32))
    rowpack = np.concatenate([
        np.asarray(inp["c"])[b].reshape(-1),
        np.asarray(inp["g_pre_mix"]).reshape(-1),
        np.asarray(inp["g_post_mix"]).reshape(-1),
        np.asarray(inp["g_pre_ffn"]).reshape(-1),
        np.asarray(inp["g_post_ffn"]).reshape(-1),
        np.asarray(inp["lam_q1"]).reshape(-1), np.asarray(inp["lam_k1"]).reshape(-1),
        np.asarray(inp["lam_q2"]).reshape(-1), np.asarray(inp["lam_k2"]).reshape(-1),
        np.asarray(inp["g_da_subln"]).reshape(-1),
    ]).reshape(1, -1)
    return {
        "x": f(np.asarray(inp["x"])[b]),
        "rowpack": f(rowpack),
        "w_ada": f(np.asarray(inp["w_ada"])[0]),
        "b_ada": f(np.asarray(inp["b_ada"])[0].reshape(1, -1)),
        "w_in": f(np.asarray(inp["w_in"])[0]),
        "w_out": f(np.asarray(inp["w_out"])[0]),
        "w_up": f(np.asarray(inp["w_up"])[0]),
        "conv_w": f(np.asarray(inp["conv_w"])[0]),
        "conv_b": f(np.asarray(inp["conv_b"])[0].reshape(1, -1)),
        "w_down": f(np.asarray(inp["w_down"])[0]),
    }


def kernel(**inputs):
    if "nc" not in _CACHE:
        _CACHE["nc"] = build()
    nc = _CACHE["nc"]
    in_maps = [_inputs_for_core(b, inputs) for b in range(8)]
    res = run_bass_kernel_spmd(nc, in_maps, core_ids=list(range(8)))
    out = np.stack([np.asarray(r["y"]) for r in res.results], axis=0)
    return out.astype(np.float32)
```

```python
import math
import numpy as np
import concourse.bass as bass
import concourse.mybir as mybir
from concourse.bass_utils import run_bass_kernel_spmd

F32 = mybir.dt.float32
BF16 = mybir.dt.bfloat16
ALU = mybir.AluOpType
AF = mybir.ActivationFunctionType
AX = mybir.AxisListType

S_LEN = 4096
D = 1024
INW = 3072
DFF = 2816
NFC = 22
NEG = -1.0e6


class Buf:
    def __init__(self, name):
        self.name = name
        self.w = None
        self.r = []
        self.al = []
        self.dsem = None
        self.dcnt = 0


def alias(a, b):
    if b not in a.al:
        a.al.append(b)
    if a not in b.al:
        b.al.append(a)


class EngQ:
    def __init__(self, name, sem):
        self.name = name
        self.sem = sem
        self.n = 0
        self.ops = []
        self.waited = {}


class Sched:
    ENGS = ("pe", "act", "dve", "pool", "sp")

    def __init__(self, nc, stack):
        self.nc = nc
        self.stack = stack
        self.q = {}
        for e in self.ENGS:
            sem = stack.enter_context(nc.semaphore("s_" + e))
            self.q[e] = EngQ(e, sem)

    def _deps(self, reads, writes):
        raw, other = [], []
        for b in reads:
            for bb in [b] + b.al:
                if bb.w is not None:
                    raw.append(bb.w)
        for b in writes:
            for bb in [b] + b.al:
                if bb.w is not None:
                    other.append(bb.w)
                other.extend(bb.r)
        return raw, other

    def _waits(self, q, raw, other):
        need = {}
        for (sem, val) in raw:
            if sem is q.sem and q.name == "pe":
                continue
            if q.waited.get(sem, 0) < val:
                need[sem] = max(need.get(sem, 0), val)
        for (sem, val) in other:
            if sem is q.sem and q.name == "pe":
                continue
            if q.waited.get(sem, 0) < val:
                need[sem] = max(need.get(sem, 0), val)
        for sem, val in need.items():
            q.waited[sem] = val
        return list(need.items())

    def op(self, eng, fn, reads=(), writes=()):
        q = self.q[eng]
        raw, other = self._deps(reads, writes)
        waits = self._waits(q, raw, other)
        q.n += 1
        tk = (q.sem, q.n)
        q.ops.append((waits, fn, (q.sem, 1)))
        for b in reads:
            b.r.append(tk)
            if len(b.r) > 64:
                b.r = self._compact(b.r)
        for b in writes:
            b.w = tk
            b.r = []
        return tk

    @staticmethod
    def _compact(lst):
        best = {}
        for sem, val in lst:
            k = id(sem)
            if k not in best or best[k][1] < val:
                best[k] = (sem, val)
        return list(best.values())

    def dma(self, eng, out, in_, reads=(), writes=(), track=None, **kw):
        q = self.q[eng]
        raw, other = self._deps(reads, writes)
        waits = self._waits(q, raw + other, [])
        tb = track or (writes[0] if writes else reads[0])
        if tb.dsem is None:
            tb.dsem = self.stack.enter_context(self.nc.semaphore("d_" + tb.name))
        tb.dcnt += 16
        tk = (tb.dsem, tb.dcnt)

        def fn(e, out=out, in_=in_, kw=kw):
            return e.dma_start(out=out, in_=in_, **kw)

        q.ops.append((waits, fn, (tb.dsem, 16)))
        for b in reads:
            b.r.append(tk)
        for b in writes:
            b.w = tk
            b.r = []
        return tk

    def final_wait(self, eng, bufs):
        q = self.q[eng]
        deps = []
        for b in bufs:
            if b.w is not None:
                deps.append(b.w)
            deps.extend(b.r)
        waits = self._waits(q, deps, [])
        q.ops.append((waits, None, None))

    def emit(self):
        nc = self.nc
        with nc.Block() as block:
            def run(q):
                def body(e):
                    for waits, fn, inc in q.ops:
                        for sem, val in waits:
                            e.wait_ge(sem, val)
                        if fn is not None:
                            fn(e).then_inc(inc[0], inc[1])
                return body
            block.tensor(run(self.q["pe"]))
            block.scalar(run(self.q["act"]))
            block.vector(run(self.q["dve"]))
            block.gpsimd(run(self.q["pool"]))
            block.sync(run(self.q["sp"]))


class T:
    def __init__(self, h, name, lo=0, hi=0):
        self.h = h
        self.b = Buf(name)
        self.lo = lo
        self.hi = hi

    def __getitem__(self, k):
        return self.h[k]


class Alloc:
    BASE = 16512
    END = 229376

    def __init__(self, nc):
        self.nc = nc
        self.cur = self.BASE
        self.cnt = 0

    @staticmethod
    def nbytes(shape, dt):
        n = 1
        for s in shape[1:]:
            n *= s
        return n * (4 if dt == F32 else 2)

    def at(self, name, shape, dt, off):
        self.cnt += 1
        h = self.nc.alloc_sbuf_tensor_at("%s_%d" % (name, self.cnt), shape, dt, offset=off)
        return T(h, name, off, off + self.nbytes(shape, dt))

    def new(self, name, shape, dt):
        off = (self.cur + 31) // 32 * 32
        t = self.at(name, shape, dt, off)
        self.cur = t.hi
        assert self.cur <= self.END, ("SBUF overflow", name, self.cur)
        return t


class Arena:
    def __init__(self, alloc, size):
        self.alloc = alloc
        self.lo = (alloc.cur + 31) // 32 * 32
        self.hi = self.lo + size
        alloc.cur = self.hi
        assert alloc.cur <= alloc.END, ("SBUF overflow arena", alloc.cur)
        self.sets = {}
        self.curs = {}

    def new(self, setname, name, shape, dt):
        cur = self.curs.get(setname, self.lo)
        off = (cur + 31) // 32 * 32
        t = self.alloc.at(name, shape, dt, off)
        assert t.hi <= self.hi, ("arena overflow", setname, name, t.hi - self.lo)
        self.curs[setname] = t.hi
        for sn, lst in self.sets.items():
            if sn == setname:
                continue
            for o in lst:
                if o.lo < t.hi and t.lo < o.hi:
                    alias(o.b, t.b)
        self.sets.setdefault(setname, []).append(t)
        return t


def build(NSB=8, dbg=None, stop_after=None):
    from contextlib import ExitStack
    nc = bass.Bass("TRN2", target_bir_lowering=False)

    def din(name, shape):
        return nc.dram_tensor(name, shape, F32, kind="ExternalInput").ap()

    x = din("x", [S_LEN, D])
    rowpack = din("rowpack", [1, 5504])
    w_ada = din("w_ada", [D, 6 * D])
    b_ada = din("b_ada", [1, 6 * D])
    w_in = din("w_in", [D, INW])
    w_out = din("w_out", [D, D])
    w_up = din("w_up", [D, 2 * DFF])
    conv_w = din("conv_w", [3, 2 * DFF])
    conv_b = din("conv_b", [1, 2 * DFF])
    w_down = din("w_down", [DFF, D])
    y = nc.dram_tensor("y", [S_LEN, D], F32, kind="ExternalOutput").ap()
    dbg_out = {}
    if dbg:
        for k, shp in dbg.items():
            dbg_out[k] = nc.dram_tensor("dbg_" + k, shp, F32, kind="ExternalOutput").ap()

    s_win = nc.dram_tensor("s_win", [D, INW], BF16).ap()
    s_wout = nc.dram_tensor("s_wout", [D, D], BF16).ap()
    s_wup = nc.dram_tensor("s_wup", [D, 11, 512], BF16).ap()
    s_wdn = nc.dram_tensor("s_wdn", [DFF, D], BF16).ap()
    B_swin = [Buf("swin%d" % g) for g in range(6)]
    B_swout = Buf("swout")
    B_swup = [Buf("swup%d" % g) for g in range(4)]
    B_swdn = [Buf("swdn%d" % g) for g in range(2)]

    with ExitStack() as st:
        S = Sched(nc, st)
        A = Alloc(nc)

        def psum(name, shape, dt):
            return T(st.enter_context(nc.psum_tensor(name, shape, dt)), name)

        PB = [psum("pb%d" % i, [128, 512], F32) for i in range(7)]
        PT = psum("pT", [128, 1024], BF16)
        PTf = T(PT.h.bitcast(F32), "pTf")
        PTf.b = PT.b
        gen_i = [0]
        att_i = [0]

        def abank():
            b = (PB[0], PB[1], PTf)[att_i[0] % 3]
            att_i[0] += 1
            return b

        def gbank():
            b = PB[gen_i[0] % 4]
            gen_i[0] += 1
            return b

        ffn_i = [0]

        def fbank():
            b = PB[ffn_i[0] % 7]
            ffn_i[0] += 1
            return b

        KT = A.new("KT", [128, 4, S_LEN], BF16)
        KTb = [[Buf("KT%d_%d" % (h, s)) for s in range(8)] for h in range(4)]
        VA = A.new("VA", [128, 32, 4, 130], BF16)
        VAb = [Buf("VA%d" % t) for t in range(32)]
        ident = A.new("ident", [128, 128], BF16)
        Dt = A.new("Dt", [128, 1024], F32)
        CB = A.new("CB", [128, 4, 32], F32)
        VS = A.new("VS", [128, 4, 128], F32)
        CB3 = A.new("CB3", [128, 4, 4], F32)
        Dt2 = A.new("Dt2", [128, 1024], F32)
        IDEC = A.new("IDEC", [128, 4, 64], F32)
        ODEC = A.new("ODEC", [128, 2, 64], F32)
        INDEC = A.new("INDEC", [128, 4, 64], F32)
        G64 = A.new("G64", [128, 2], F32)
        G1 = A.new("G1", [128, D], F32)
        G2 = A.new("G2", [128, D], F32)
        GS = A.new("GS", [128, 128], F32)
        colA = A.new("colA", [128, 16], F32)
        colB = A.new("colB", [128, 16], F32)
        convc = A.new("convc", [128, 4, 2 * NFC], F32)
        HALO = A.new("HALO", [128, 2, 2 * NFC, 2], F32)
        HALOb = [[Buf("halo%d_%d" % (p_, c_)) for c_ in range(2 * NFC)] for p_ in range(2)]
        Sst = A.new("Sst", [128, 2, 128], F32)
        Sbd = A.new("Sbd", [128, 2, 128], BF16)
        neglam = A.new("neglam", [128, 1], F32)
        ones1 = A.new("ones1", [1, 128], F32)
        epsc = A.new("epsc", [128, 1], F32)
        sccol = A.new("sccol", [128, 8], BF16)
        stat = A.new("stat", [128, 64], F32)
        statb = [Buf("stat%d" % i) for i in range(64)]
        ln8 = A.new("ln8", [128, 1], F32)
        ln05 = A.new("ln05", [128, 1], F32)
        idx63 = A.new("idx63", [128, 1], F32)
        lam_t = A.new("lam_t", [1, 8], F32)
        WS = [A.new("WS%d" % i, [128, 8, 512], BF16) for i in range(2)]
        hT = A.new("hT", [128, 8, 512], BF16)
        XS = [A.new("XS%d" % i, [128, D], F32) for i in range(2)]
        XN = [A.new("XN%d" % i, [128, D], BF16) for i in range(2)]
        junk = A.new("junk", [128, D], BF16)
        X1 = [A.new("X1_%d" % i, [128, D], F32) for i in range(4)]
        AR = Arena(A, A.END - ((A.cur + 31) // 32 * 32))
        arena_size = AR.hi - AR.lo
        print("arena size", arena_size, "persistent+work", AR.lo - A.BASE)

        qTbd = AR.new("M", "qTbd", [128, 4, 2, 512], BF16)
        rqTbd = AR.new("M", "rqTbd", [128, 2, 8, 2, 64], BF16)
        rkT = AR.new("M", "rkT", [128, 2, 512], BF16)
        rkdbd = [AR.new("M", "rkdbd%d" % t, [128, 2, 256], BF16) for t in range(4)]
        rv = [AR.new("M", "rv%d" % t, [128, 512], BF16) for t in range(4)]
        sg = [AR.new("M", "sg%d" % t, [128, 512], BF16) for t in range(4)]
        sTbd = [AR.new("M", "sTbd%d" % i, [128, 2, 256], BF16) for i in range(2)]
        tT = [AR.new("M", "tT%d" % i, [128, 512], F32) for i in range(2)]
        Pm = [AR.new("M", "Pm%d" % i, [128, 512], BF16) for i in range(4)]
        oda = [AR.new("M", "oda%d" % i, [128, 4, 128], F32) for i in range(2)]
        sig = AR.new("M", "sig", [128, 512], F32)
        araw = AR.new("M", "araw", [128, 1056], F32)
        mix = [AR.new("M", "mix%d" % t, [128, D], BF16) for t in range(4)]
        mixT = AR.new("M", "mixT", [128, 8, 512], BF16)
        actT = AR.new("F", "actT", [128, NFC, 512], BF16)
        WD = [AR.new("F", "WD%d" % i, [128, NFC, 256], BF16) for i in range(2)]
        Ug = [AR.new("F", "Ug%d" % i, [128, 514], F32) for i in range(2)]
        Uv = [AR.new("F", "Uv%d" % i, [128, 514], F32) for i in range(2)]
        Yg = [AR.new("F", "Yg%d" % i, [128, 512], F32) for i in range(2)]
        Yv = [AR.new("F", "Yv%d" % i, [128, 512], F32) for i in range(2)]
        UH = {id(u_): Buf("uh_" + u_.b.name) for u_ in Ug + Uv}
        fsb = [AR.alloc.at("fsb%d" % t, [128, D], F32, Ug[0].lo + t * 4096) for t in range(4)]
        for u_ in Ug + Uv:
            for o in list(u_.b.al):
                alias(UH[id(u_)], o)
        for ft in fsb:
            assert ft.hi <= AR.hi
            for o in Ug + Uv + Yg + Yv:
                if o.lo < ft.hi and ft.lo < o.hi:
                    alias(o.b, ft.b)
                    if id(o) in UH:
                        alias(UH[id(o)], ft.b)
            for o in AR.sets["M"]:
                if o.lo < ft.hi and ft.lo < o.hi:
                    alias(o.b, ft.b)
        print("set sizes", {k: v - AR.lo for k, v in AR.curs.items()})
        rowp = AR.new("P", "rowp", [1, 5504], F32)
        wast = [AR.new("P", "wast%d" % i, [128, 8, 256], BF16) for i in range(3)]
        bada = [AR.new("P", "bada%d" % i, [1, 1024], F32) for i in range(2)]
        modp = [AR.new("P", "modp%d" % i, [1, 1024], F32) for i in range(2)]
        crow = [AR.new("P", "crow%d" % i, [1, 1408], F32) for i in range(1)]
        tmpr = AR.new("P", "tmpr", [1, 1024], F32)
        scrow = AR.new("P", "scrow", [1, 1024], F32)

        actTb = [Buf("actT%d" % c_) for c_ in range(NFC)]
        for b_ in actTb:
            for o in list(actT.b.al):
                alias(b_, o)
        mixTb = [Buf("mixT%d" % t_) for t_ in range(4)]
        for b_ in mixTb:
            for o in list(mixT.b.al):
                alias(b_, o)

        def OP(eng, fn, r=(), w=()):
            return S.op(eng, fn, [t.b if isinstance(t, T) else t for t in r],
                        [t.b if isinstance(t, T) else t for t in w])

        def DMA(eng, out, in_, r=(), w=(), **kw):
            return S.dma(eng, out, in_, [t.b if isinstance(t, T) else t for t in r],
                         [t.b if isinstance(t, T) else t for t in w], **kw)

        def mm(out, lhsT, rhs, start, stop, r, w, skip=False):
            if skip:
                return OP("pe", lambda e: e.matmul(out, lhsT=lhsT, rhs=rhs, start=start, stop=stop,
                                                   skip_group_check=True), r, w)
            return OP("pe", lambda e: e.matmul(out, lhsT=lhsT, rhs=rhs, start=start, stop=stop), r, w)

        def act(out, in_, func, r, w, **kw):
            return OP("act", lambda e: e.activation(out, in_, func, **kw), r, w)

        def ts(eng, out, in0, s1, s2, op0, op1, r, w):
            if s2 is None:
                return OP(eng, lambda e: e.tensor_scalar(out, in0, s1, None, op0), r, w)
            return OP(eng, lambda e: e.tensor_scalar(out, in0, s1, s2, op0, op1), r, w)

        def stt(eng, out, in0, sc, in1, op0, op1, r, w):
            return OP(eng, lambda e: e.scalar_tensor_tensor(out, in0, sc, in1, op0, op1), r, w)

        def tt(eng, out, in0, in1, op, r, w):
            return OP(eng, lambda e: e.tensor_tensor(out, in0, in1, op), r, w)

        def cp(eng, out, in_, r, w):
            if eng == "act":
                return act(out, in_, AF.Copy, r, w)
            return OP(eng, lambda e: e.tensor_copy(out, in_), r, w)

        def recip(out, in_, r, w):
            return OP("dve", lambda e: e.reciprocal(out, in_), r, w)

        def memset(eng, ap, val, w):
            return OP(eng, lambda e: e.memset(ap, val), (), w)

        def rstd_from(dst, src, n, bufs_r, bufs_w):
            act(dst, src, AF.Ln, list(bufs_r) + [epsc], bufs_w, scale=1.0 / n, bias=epsc[:])
            act(dst, dst, AF.Exp, bufs_w, bufs_w, scale=-0.5)

        def cast_win(g):
            DMA("pool", s_win[:, g * 512:(g + 1) * 512], w_in[:, g * 512:(g + 1) * 512], w=[B_swin[g]])

        def early_casts():
            for g in range(1, 6):
                cast_win(g)
            for hf in range(2):
                DMA("pool", s_wout[:, hf * 512:(hf + 1) * 512], w_out[:, hf * 512:(hf + 1) * 512], w=[B_swout])

        cast_win(0)
        def deferred_casts():
            for g in range(11):
                bb = B_swup[min(g // 3, 3)]
                DMA("pool", s_wup[:, g, 0:256], w_up[:, g * 256:(g + 1) * 256], w=[bb])
                DMA("pool", s_wup[:, g, 256:512], w_up[:, DFF + g * 256:DFF + (g + 1) * 256], w=[bb])
            for qq in range(4):
                DMA("pool", s_wdn[:, qq * 256:(qq + 1) * 256], w_down[:, qq * 256:(qq + 1) * 256],
                    w=[B_swdn[qq // 2]])

        DMA("sp", rowp[:], rowpack, w=[rowp])
        memset("pool", ones1[:], 1.0, [ones1])
        memset("pool", epsc[:], 1e-6, [epsc])
        memset("pool", ln05[:], math.log(0.5), [ln05])
        idf = tT[0]
        OP("pool", lambda e: e.iota(idf[:, 0:128], [[1, 128]], base=0, channel_multiplier=-1,
                                    allow_small_or_imprecise_dtypes=True), (), [idf])
        OP("pool", lambda e: e.tensor_single_scalar(ident[:], idf[:, 0:128], 0.0, ALU.is_equal), [idf], [ident])
        OP("pool", lambda e: e.iota(Dt[:], [[1, 1024]], base=-384, channel_multiplier=-1,
                                    allow_small_or_imprecise_dtypes=True), (), [Dt])
        act(Dt[:], Dt[:], AF.Abs, [Dt], [Dt])
        ts("dve", Dt[:], Dt[:], -1.0, None, ALU.mult, None, [Dt], [Dt])
        memset("pool", Dt[64:128, 384:448], NEG, [Dt])
        slopes = [2.0 ** (-8.0 * (h + 1) / 4) for h in range(4)]
        for h in range(4):
            OP("pool", lambda e, h=h: e.iota(CB[:, h, :], [[1, 32]], base=-1, channel_multiplier=0,
                                             allow_small_or_imprecise_dtypes=True), (), [CB])
        for h in range(4):
            ts("dve", CB[:, h, :], CB[:, h, :], -slopes[h] * 128.0, None, ALU.mult, None, [CB], [CB])
        for h in range(4):
            OP("pool", lambda e, h=h: e.iota(CB3[:, h, :], [[128, 4]], base=0, channel_multiplier=0,
                                             allow_small_or_imprecise_dtypes=True), (), [CB3])
        for h in range(4):
            ts("dve", CB3[:, h, :], CB3[:, h, :], slopes[h], None, ALU.mult, None, [CB3], [CB3])
        OP("pool", lambda e: e.iota(VS[:, 1, :], [[0, 128]], base=0, channel_multiplier=1,
                                    allow_small_or_imprecise_dtypes=True), (), [VS])
        for h in (3, 2, 1):
            act(VS[:, h, :], VS[:, 1, :], AF.Exp, [VS], [VS], scale=slopes[h])
        memset("pool", VS[:, 0, :], 1.0, [VS])
        OP("pool", lambda e: e.iota(Dt2[:], [[1, 1024]], base=-384, channel_multiplier=-1,
                                    allow_small_or_imprecise_dtypes=True), (), [Dt2])
        tt("dve", Dt2[:], Dt2[:], Dt[:], ALU.add, [Dt2, Dt], [Dt2])
        lg = [math.log(1.0 - 2.0 ** (-5.0 - h)) for h in range(4)]
        memset("pool", ln8[:], math.log(0.125), [ln8])
        for half in range(2):
            ps_ = slice(half * 64, half * 64 + 64)
            OP("pool", lambda e, ps_=ps_: e.iota(IDEC[ps_, 0, :], [[1, 64]], base=0, channel_multiplier=-1,
                                                 allow_small_or_imprecise_dtypes=True), (), [IDEC])
            OP("pool", lambda e, ps_=ps_: e.iota(INDEC[ps_, 0, 0:1], [[1, 1]], base=63, channel_multiplier=-1,
                                                 allow_small_or_imprecise_dtypes=True), (), [INDEC])
        act(IDEC[:, 0, :], IDEC[:, 0, :], AF.Abs, [IDEC], [IDEC])
        OP("pool", lambda e: e.iota(IDEC[:, 1, :], [[1, 64]], base=1, channel_multiplier=0,
                                    allow_small_or_imprecise_dtypes=True), (), [IDEC])
        tt("dve", IDEC[:, 3, :], IDEC[:, 0, :], IDEC[:, 1, :], ALU.subtract, [IDEC], [IDEC])
        for h in (0, 1, 2, 3):
            act(IDEC[:, h, :], IDEC[:, 3, :], AF.Exp, [IDEC, ln8], [IDEC], scale=lg[h], bias=ln8[:])
        cp("dve", idx63[:], INDEC[:, 0, 0:1], [INDEC], [idx63])
        for h in range(4):
            act(INDEC[:, h, :], idx63[:].to_broadcast([128, 64]), AF.Exp, [idx63, ln8], [INDEC],
                scale=lg[h], bias=ln8[:])
        OP("pool", lambda e: e.iota(ODEC[:, 0, :], [[1, 64]], base=1, channel_multiplier=0,
                                    allow_small_or_imprecise_dtypes=True), (), [ODEC])
        for a_ in (1, 0):
            for hp in range(2):
                ps_ = slice(hp * 64, hp * 64 + 64)
                act(ODEC[ps_, a_, :], ODEC[ps_, 0, :], AF.Exp, [ODEC], [ODEC], scale=lg[2 * a_ + hp])
        for a_ in range(2):
            for hp in range(2):
                memset("pool", G64[hp * 64:(hp + 1) * 64, a_:a_ + 1], math.exp(lg[2 * a_ + hp] * 64.0), [G64])
        memset("pool", Sst[:], 0.0, [Sst])
        memset("pool", Sbd[:], 0.0, [Sbd])
        memset("pool", HALO[:], 0.0, [b_ for l_ in HALOb for b_ in l_])
        memset("pool", VA[:], 1.0, VAb)
        for h in range(1, 4):
            ts("dve", VA[:, :, h, 128:129], VA[:, :, h, 128:129], VS[:, h, 0:1], None, ALU.mult, None,
               VAb + [VS], VAb)

        crow_c = rowp[0:1, 0:1024]
        act(tmpr[:], crow_c, AF.Exp, [rowp], [tmpr], scale=-1.0)
        ts("dve", tmpr[:], tmpr[:], 1.0, None, ALU.add, None, [tmpr], [tmpr])
        recip(tmpr[:], tmpr[:], [tmpr], [tmpr])
        tt("dve", scrow[:], tmpr[:], crow_c, ALU.mult, [tmpr, rowp], [scrow])

        def row_to_cols(row_ap_fn, n, dst, dst_col0, rbufs, wbuf):
            pb = gbank()
            for j in range(n):
                mm(pb[:, j:j + 1], row_ap_fn(j), ones1[0:1, 0:1], True, True, list(rbufs) + [ones1], [pb])
            cp("dve", dst[:, dst_col0:dst_col0 + n], pb[:, 0:n], [pb], [wbuf])

        row_to_cols(lambda j: scrow[0:1, j * 128:(j + 1) * 128], 8, sccol, 0, [scrow], sccol)

        wi = [0]
        for cb in range(2):
            bpiece = bada[cb % 2]
            mpiece = modp[cb % 2]
            DMA("sp", bpiece[:], b_ada[0:1, cb * 1024:(cb + 1) * 1024], w=[bpiece])
            for sub in range(4):
                c0 = cb * 1024 + sub * 256
                wsb = wast[wi[0] % 3]
                wi[0] += 1
                DMA("pool", wsb[:], w_ada[:, c0:c0 + 256].rearrange("(k p) n -> p k n", p=128), w=[wsb])
                pb = gbank()
                for k in range(8):
                    mm(pb[0:1, 0:256], sccol[:, k:k + 1], wsb[:, k, :], k == 0, k == 7, [sccol, wsb], [pb])
                tt("dve", mpiece[0:1, sub * 256:(sub + 1) * 256], pb[0:1, 0:256],
                   bpiece[0:1, sub * 256:(sub + 1) * 256], ALU.add, [pb, bpiece], [mpiece])
            gcol = {1: 1024, 2: 2048, 4: 3072, 5: 4096}
            if cb in (0, 3):
                dst = colA if cb == 0 else colB
                row_to_cols(lambda j, mpiece=mpiece: mpiece[0:1, j * 128:(j + 1) * 128], 8, dst, 8, [mpiece], dst)
            elif cb in (1, 4):
                dst = colA if cb == 1 else colB
                stt("dve", tmpr[:], mpiece[:], 1.0, rowp[0:1, gcol[cb]:gcol[cb] + 1024], ALU.add, ALU.mult,
                    [mpiece, rowp], [tmpr])
                row_to_cols(lambda j: tmpr[0:1, j * 128:(j + 1) * 128], 8, dst, 0, [tmpr], dst)
            else:
                dst = G1 if cb == 2 else G2
                tt("dve", tmpr[:], mpiece[:], rowp[0:1, gcol[cb]:gcol[cb] + 1024], ALU.mult, [mpiece, rowp], [tmpr])
                for hf in range(2):
                    pb = gbank()
                    mm(pb[:], ones1[0:1, :], tmpr[0:1, hf * 512:(hf + 1) * 512], True, True, [ones1, tmpr], [pb])
                    cp("act", dst[:, hf * 512:(hf + 1) * 512], pb[:], [pb], [dst])

        early_casts()
        lq = 5120
        for i in range(2):
            tt("dve", tmpr[0:1, 0:64], rowp[0:1, lq + i * 128:lq + i * 128 + 64],
               rowp[0:1, lq + i * 128 + 64:lq + i * 128 + 128], ALU.mult, [rowp], [tmpr])
            OP("dve", lambda e, i=i: e.reduce_sum(lam_t[0:1, i:i + 1], tmpr[0:1, 0:64], axis=AX.X), [tmpr], [lam_t])
        act(lam_t[0:1, 2:4], lam_t[0:1, 0:2], AF.Exp, [lam_t], [lam_t])
        tt("dve", lam_t[0:1, 4:5], lam_t[0:1, 3:4], lam_t[0:1, 2:3], ALU.subtract, [lam_t], [lam_t])
        ts("dve", lam_t[0:1, 5:6], lam_t[0:1, 4:5], -0.2, None, ALU.add, None, [lam_t], [lam_t])
        pb = gbank()
        mm(pb[:, 0:1], ones1[0:1, :], lam_t[0:1, 5:6], True, True, [ones1, lam_t], [pb])
        cp("dve", neglam[:], pb[:, 0:1], [pb], [neglam])
        pb = gbank()
        mm(pb[:, 0:128], ones1[0:1, :], rowp[0:1, 5376:5504], True, True, [ones1, rowp], [pb])
        act(GS[:], pb[:, 0:128], AF.Copy, [pb], [GS], scale=0.8)
        dwast = []
        for i in range(3):
            t_ = A.at("dwast%d" % i, [128, 8, 256], BF16, X1[i].lo)
            alias(t_.b, X1[i].b)
            dwast.append(t_)
        drow = []
        for i in range(4):
            t_ = A.at("drow%d" % i, [1, 256], F32, X1[3].lo + i * 1024)
            alias(t_.b, X1[3].b)
            drow.append(t_)

        def mod_gen():
            goff = {2: 2048, 4: 3072, 5: 4096}
            for p_ in range(16):
                cb = 2 + p_ // 4
                sub = p_ % 4
                c0 = cb * 1024 + sub * 256
                wsb = dwast[p_ % 3]
                DMA("pool", wsb[:], w_ada[:, c0:c0 + 256].rearrange("(k p) n -> p k n", p=128), w=[wsb])
                brow, grow, mrow = drow[0], drow[1], drow[2 + p_ % 2]
                DMA("sp", brow[:], b_ada[0:1, c0:c0 + 256], w=[brow])
                if cb in goff:
                    DMA("sp", grow[:], rowpack[0:1, goff[cb] + sub * 256:goff[cb] + (sub + 1) * 256], w=[grow])
                yield
                pb = gbank()
                for k in range(8):
                    mm(pb[0:1, 0:256], sccol[:, k:k + 1], wsb[:, k, :], k == 0, k == 7, [sccol, wsb], [pb])
                tt("dve", mrow[:], pb[0:1, 0:256], brow[:], ALU.add, [pb, brow], [mrow])
                if cb == 3:
                    row_to_cols(lambda j, mrow=mrow: mrow[0:1, j * 128:(j + 1) * 128], 2, colB, 8 + 2 * sub, [mrow], colB)
                elif cb == 4:
                    stt("dve", mrow[:], mrow[:], 1.0, grow[:], ALU.add, ALU.mult, [mrow, grow], [mrow])
                    row_to_cols(lambda j, mrow=mrow: mrow[0:1, j * 128:(j + 1) * 128], 2, colB, 2 * sub, [mrow], colB)
                else:
                    dst = G1 if cb == 2 else G2
                    tt("dve", mrow[:], mrow[:], grow[:], ALU.mult, [mrow, grow], [mrow])
                    pb2 = gbank()
                    mm(pb2[:, 0:256], ones1[0:1, :], mrow[0:1, :], True, True, [ones1, mrow], [pb2])
                    cp("act", dst[:, sub * 256:(sub + 1) * 256], pb2[:, 0:256], [pb2], [dst])
                yield

        modg = mod_gen()

        dcrow = []
        for i in range(2):
            t_ = A.at("dcrow%d" % i, [1, 1408], F32, X1[2 * i].lo)
            alias(t_.b, X1[2 * i].b)
            alias(t_.b, X1[2 * i + 1].b)
            for o in dwast + drow:
                if o.lo < t_.hi and t_.lo < o.hi:
                    alias(t_.b, o.b)
            dcrow.append(t_)

        def conv_cols():
            ci_ = 0
            for r_ in range(4):
                for pc in range(4):
                    cr = dcrow[ci_ % 2]
                    ci_ += 1
                    src = (conv_w[r_:r_ + 1, pc * 1408:(pc + 1) * 1408] if r_ < 3
                           else conv_b[0:1, pc * 1408:(pc + 1) * 1408])
                    DMA("sp", cr[:], src, w=[cr])
                    pb = gbank()
                    for j in range(11):
                        mm(pb[:, j:j + 1], cr[0:1, j * 128:(j + 1) * 128], ones1[0:1, 0:1], True, True,
                           [cr, ones1], [pb])
                    cp("dve", convc[:, r_, pc * 11:(pc + 1) * 11], pb[:, 0:11], [pb], [convc])


        pend_epi = []
        pref_w = {}
        ws_i = [0]
        xs_i = [0]
        xn_i = [0]

        def load_w(src_ap, rbuf):
            wsb = WS[ws_i[0] % 2]
            ws_i[0] += 1
            DMA("sp", wsb[:], src_ap, r=[rbuf], w=[wsb])
            return wsb

        def norm_tile(t, src, sb_, col):
            sq = statb[t]
            act(junk[:], src, AF.Square, [sb_], [junk, sq], accum_out=stat[:, t:t + 1])
            rstd_from(stat[:, 4 + t:5 + t], stat[:, t:t + 1], D, [sq], [statb[4 + t]])
            xnb = XN[xn_i[0] % 2]
            xn_i[0] += 1
            act(xnb[:], src, AF.Copy, [sb_, statb[4 + t]], [xnb], scale=stat[:, 4 + t:5 + t])
            for k in range(8):
                OP("pe", lambda e, k=k, xnb=xnb: e.transpose(PT[:, k * 128:(k + 1) * 128],
                                                              xnb[:, k * 128:(k + 1) * 128], ident[:]),
                   [xnb, ident], [PT])
            for k in range(8):
                ts("dve", hT[:, k, t * 128:(t + 1) * 128], PT[:, k * 128:(k + 1) * 128],
                   col[:, k:k + 1], col[:, 8 + k:9 + k], ALU.mult, ALU.add, [PT, col], [hT])

        def rezero():
            memset("pool", qTbd[:], 0.0, [qTbd])
            memset("pool", rqTbd[:], 0.0, [rqTbd])
            for t in range(4):
                memset("pool", rkdbd[t][:], 0.0, [rkdbd[t]])
            for i in range(2):
                memset("pool", sTbd[i][:], 0.0, [sTbd[i]])

        def phaseA_tile(sbn, t):
            xb = XS[xs_i[0] % 2]
            xs_i[0] += 1
            DMA("sp", xb[:], x[sbn * 512 + t * 128:sbn * 512 + (t + 1) * 128, :], w=[xb])
            norm_tile(t, xb[:], xb, colA)

        def phaseA(sbn):
            for t in range(4):
                phaseA_tile(sbn, t)

        class _Stop(Exception):
            pass

        def chk(tag):
            if stop_after == tag:
                raise _Stop()

        def _main():
          for sb in range(NSB):
              t0 = sb * 512
              if sb == 0:
                  phaseA(0)

              if stop_after == 'A':
                  break
              if sb == 0:
                  rezero()

              ev = [0]

              def ev_eng():
                  ev[0] += 1
                  return "act"

              def fm_chunk(wsb, jc, consume):
                  if sb == 0:
                      next(modg, None)
                  pb = gbank()
                  for k in range(8):
                      mm(pb[:], wsb[:, k, jc * 128:(jc + 1) * 128], hT[:, k, :], k == 0, k == 7, [wsb, hT], [pb])
                  consume(pb)

              def tm_tile(wsb, t, c0, n, consume):
                  if sb == 0:
                      next(modg, None)
                  pb = gbank()
                  for k in range(8):
                      mm(pb[:, 0:n], hT[:, k, t * 128:(t + 1) * 128], wsb[:, k, c0:c0 + n], k == 0, k == 7,
                         [wsb, hT], [pb])
                  consume(pb)

              def wgrp(g):
                  if g >= 1 and pend_epi:
                      pend_epi.pop(0)()
                  if g in pref_w:
                      return pref_w.pop(g)
                  return load_w(s_win[:, g * 512:(g + 1) * 512].rearrange("(k p) n -> p k n", p=128), B_swin[g])

              wsb = wgrp(0)
              for h in range(4):
                  def cons(pb, h=h):
                      cp("act", qTbd[0:64, h, 0, :], pb[0:64, :], [pb], [qTbd])
                      cp("dve", qTbd[64:128, h, 1, :], pb[64:128, :], [pb], [qTbd])
                  fm_chunk(wsb, h, cons)
              chk('B0')
              wsb = wgrp(1)
              for h in range(4):
                  def cons(pb, h=h):
                      cp(ev_eng(), KT[:, h, t0:t0 + 512], pb[:], [pb], [KTb[h][sb]])
                  fm_chunk(wsb, h, cons)
              chk('B1')
              wsb = wgrp(2)
              for t in range(4):
                  def cons(pb, t=t):
                      tt("dve", VA[:, sb * 4 + t, :, 0:128], pb[:].rearrange("p (h e) -> p h e", e=128), VS[:],
                         ALU.mult, [pb, VS], [VAb[sb * 4 + t]])
                  tm_tile(wsb, t, 0, 512, cons)
              chk('B2')
              wsb = wgrp(3)
              for a_ in range(2):
                  def cons(pb, a_=a_):
                      for hp in range(2):
                          ps_ = slice(hp * 64, hp * 64 + 64)
                          for n in range(8):
                              tt("dve", rqTbd[ps_, a_, n, hp, :], pb[ps_, n * 64:(n + 1) * 64], ODEC[ps_, a_, :],
                                 ALU.mult, [pb, ODEC], [rqTbd])
                  fm_chunk(wsb, a_, cons)
              for a_ in range(2):
                  def cons(pb, a_=a_):
                      cp(ev_eng(), rkT[:, a_, :], pb[:], [pb], [rkT])
                  fm_chunk(wsb, 2 + a_, cons)
              for t in range(4):
                  def cons(pb, t=t):
                      ind = INDEC[:].rearrange("p h d -> p (h d)")
                      tt("dve", rkdbd[t][0:64, 0, :], pb[0:64, 0:256], ind[0:64, :], ALU.mult, [pb, INDEC], [rkdbd[t]])
                      tt("dve", rkdbd[t][64:128, 1, :], pb[64:128, 0:256], ind[64:128, :], ALU.mult,
                         [pb, INDEC], [rkdbd[t]])
                  tm_tile(wsb, t, 256, 256, cons)
              chk('B3')
              wsb = wgrp(4)
              for t in range(4):
                  def cons(pb, t=t):
                      cp(ev_eng(), rv[t][:], pb[:], [pb], [rv[t]])
                  tm_tile(wsb, t, 0, 512, cons)
              chk('B4')
              wsb = wgrp(5)
              for t in range(4):
                  def cons(pb, t=t):
                      act(sig[:], pb[:], AF.Tanh, [pb], [sig], scale=0.5)
                      stt("dve", sg[t][:], sig[:], 1.0, pb[:], ALU.add, ALU.mult, [sig, pb], [sg[t]])
                  tm_tile(wsb, t, 0, 512, cons)

              if stop_after == 'B':
                  break
              if sb == 0:
                  for _ in modg:
                      pass
                  deferred_casts()
              def ret_gen():
                  rb = statb[60]
                  idec = IDEC[:].rearrange("p h i -> p (h i)")
                  for t in range(4):
                      psS = PB[2]
                      for c_ in range(2):
                          n = 2 * t + c_
                          for a_ in range(2):
                              mm(psS[c_ * 64:(c_ + 1) * 64, a_ * 128:(a_ + 1) * 128].rearrange("p (b i) -> p b i", b=2),
                                 rkT[:, a_, n * 64:(n + 1) * 64], rqTbd[:, a_, n, :, :], True, True, [rkT, rqTbd], [psS])
                      yield
                      stb = sTbd[t % 2]
                      for c_ in range(2):
                          tt("dve", stb[c_ * 64:(c_ + 1) * 64, c_, :], psS[c_ * 64:(c_ + 1) * 64, 0:256],
                             idec[c_ * 64:(c_ + 1) * 64, :], ALU.mult, [psS, IDEC], [stb])
                      yield
                      psO = PB[3]
                      for c_ in range(2):
                          for h in range(4):
                              mm(psO[c_ * 64:(c_ + 1) * 64, h * 128:(h + 1) * 128], stb[:, c_, h * 64:(h + 1) * 64],
                                 rv[t][:, h * 128:(h + 1) * 128], h == 0, False, [stb, rv[t]], [psO], skip=True)
                      for c_ in range(2):
                          n = 2 * t + c_
                          for h in range(4):
                              mm(psO[c_ * 64:(c_ + 1) * 64, h * 128:(h + 1) * 128],
                                 rqTbd[:, h // 2, n, h % 2, :], Sbd[:, h // 2, :], False, False, [rqTbd, Sbd], [psO],
                                 skip=True)
                          psU = PB[2]
                          for h in range(4):
                              mm(psU[(h % 2) * 64:(h % 2) * 64 + 64, (h // 2) * 128:(h // 2) * 128 + 128],
                                 rkdbd[t][:, c_, h * 64:(h + 1) * 64], rv[t][:, h * 128:(h + 1) * 128], True, True,
                                 [rkdbd[t], rv[t]], [psU])
                          yield
                          for a_ in range(2):
                              stt("dve", Sst[:, a_, :], Sst[:, a_, :], G64[:, a_:a_ + 1], psU[:, a_ * 128:(a_ + 1) * 128],
                                  ALU.mult, ALU.add, [Sst, G64, psU], [Sst])
                          cp("dve", Sbd[:], Sst[:], [Sst], [Sbd])
                          yield
                      for h in range(4):
                          act(junk[:, 0:128], psO[:, h * 128:(h + 1) * 128], AF.Square, [psO], [junk, rb],
                              accum_out=stat[:, 56 + h:57 + h])
                      act(stat[:, 56:60], stat[:, 56:60], AF.Ln, [rb, epsc], [rb], scale=1.0 / 128, bias=epsc[:])
                      act(stat[:, 56:60], stat[:, 56:60], AF.Exp, [rb, ln05], [rb], scale=-0.5, bias=ln05[:])
                      yield
                      for h in range(4):
                          stt("dve", mix[t][:, 512 + h * 128:512 + (h + 1) * 128], psO[:, h * 128:(h + 1) * 128],
                              stat[:, 56 + h:57 + h], sg[t][:, h * 128:(h + 1) * 128], ALU.mult, ALU.mult,
                              [psO, rb, sg[t]], [mix[t]])
                      yield

              retg = ret_gen()
              n_items_total = 4 * 6 + 2 * sum(min(4 * sb, w_) for w_ in (4, 16, 64, 256))
              ret_stride = max(1, n_items_total // 30)
              ret_cnt = [0]
              nkb = 4 * sb + 4
              tt_i = [0]
              pm_i = [0]
              od_i = [0]
              pend_fin = []
              WINDOW = [4, 16, 64, 256]
              for h in range(4):
                  cscale = slopes[h] * 8.0
                  items = []
                  for kb in range(nkb):
                      i = kb - 4 * sb
                      if i < 0:
                          tbk = 4 * sb - kb
                          if tbk > WINDOW[h]:
                              continue
                          for qa in (0, 256):
                              items.append((kb, qa, qa + 256, qa + 512, tbk))
                      else:
                          lo = 128 * i
                          ranges = []
                          if lo < 256:
                              ranges.append((lo, 256))
                              ranges.append((256, 512))
                          else:
                              ranges.append((lo, 512))
                          for qa, qb in ranges:
                              items.append((kb, qa, qb, qa - 128 * i + 384, None))
                  accX, accY, accZ = PB[4], PB[5], PB[6]

                  def region(m, qt):
                      if qt < 3:
                          bank = accX if m == 0 else accY
                          return bank, qt * 132
                      return accZ, m * 132

                  started = set()
                  sbanks = {}

                  def do_qk(it):
                      kb, qa, qb, dc0, tbk = it
                      w_ = qb - qa
                      pb = abank()
                      mm(pb[:, 0:2 * w_].rearrange("p (m q) -> p m q", m=2),
                         KT[:, h, kb * 128:(kb + 1) * 128], qTbd[:, h, :, qa:qb], True, True,
                         [KTb[h][kb // 4], qTbd], [pb])
                      sbanks[it] = pb

                  def do_rest(it):
                      kb, qa, qb, dc0, tbk = it
                      w_ = qb - qa
                      pb = sbanks.pop(it)
                      pm_ = Pm[pm_i[0] % 4]
                      pm_i[0] += 1
                      if h == 0:
                          tb_ = tT[tt_i[0] % 2]
                          tt_i[0] += 1
                          for m in range(2):
                              stt("dve", tb_[:, m * w_:(m + 1) * w_], Dt[:, dc0:dc0 + w_], cscale,
                                  pb[:, m * w_:(m + 1) * w_], ALU.mult, ALU.add, [Dt, pb], [tb_])
                          if tbk is None:
                              act(pm_[:, 0:2 * w_], tb_[:, 0:2 * w_], AF.Exp, [tb_], [pm_], scale=0.125)
                          else:
                              act(pm_[:, 0:2 * w_], tb_[:, 0:2 * w_], AF.Exp, [tb_, CB], [pm_], scale=0.125,
                                  bias=CB[:, h, tbk:tbk + 1])
                      elif tbk is None:
                          ii = kb - 4 * sb
                          tb_ = tT[tt_i[0] % 2]
                          tt_i[0] += 1
                          for m in range(2):
                              stt("dve", tb_[:, m * w_:(m + 1) * w_], Dt2[:, dc0:dc0 + w_], cscale,
                                  pb[:, m * w_:(m + 1) * w_], ALU.mult, ALU.add, [Dt2, pb], [tb_])
                          act(pm_[:, 0:2 * w_], tb_[:, 0:2 * w_], AF.Exp, [tb_, CB3], [pm_], scale=0.125,
                              bias=CB3[:, h, ii:ii + 1])
                      else:
                          act(pm_[:, 0:2 * w_], pb[:, 0:2 * w_], AF.Exp, [pb], [pm_], scale=0.125,
                              bias=float(-slopes[h] * 128.0 * tbk))
                      for m in range(2):
                          for qt in range(qa // 128, qb // 128):
                              bank, c0 = region(m, qt)
                              first = id(bank) not in started
                              started.add(id(bank))
                              off = m * w_ + (qt * 128 - qa)
                              mm(bank[:, c0:c0 + 129], pm_[:, off:off + 128], VA[:, kb, h, 0:129],
                                 first, False, [pm_, VAb[kb]], [bank], skip=True)

                  for idx, it in enumerate(items):
                      if idx == 0:
                          do_qk(it)
                          if len(items) > 1:
                              do_qk(items[1])
                      if idx + 2 < len(items):
                          do_qk(items[idx + 2])
                      do_rest(it)
                      if pend_fin and (idx == 5 or idx == len(items) - 1):
                          pend_fin.pop(0)()
                      ret_cnt[0] += 1
                      if ret_cnt[0] % ret_stride == 0:
                          next(retg, None)

                  v3 = lambda ap, n: ap.rearrange("p (r c) -> p r c", c=132)[:, 0:n, 0:129]
                  cp("dve", v3(araw[:, 0:396], 3), v3(accX[:, 0:396], 3), [accX], [araw])
                  cp("dve", v3(araw[:, 396:792], 3), v3(accY[:, 0:396], 3), [accY], [araw])
                  cp("dve", v3(araw[:, 792:1056], 2), v3(accZ[:, 0:264], 2), [accZ], [araw])

                  def finalize(h=h):
                      ob = oda[od_i[0] % 2]
                      od_i[0] += 1
                      sc0 = 8 + h * 12
                      colsb = [statb[8 + h]]

                      def rreg(m, qt):
                          return (m * 396 + qt * 132) if qt < 3 else (792 + m * 132)

                      recip(stat[:, sc0:sc0 + 3], araw[:, 0:396].rearrange("p (r c) -> p r c", c=132)[:, :, 128],
                            [araw], colsb)
                      recip(stat[:, sc0 + 4:sc0 + 7], araw[:, 396:792].rearrange("p (r c) -> p r c", c=132)[:, :, 128],
                            [araw], colsb)
                      recip(stat[:, sc0 + 3:sc0 + 4], araw[:, 920:921], [araw], colsb)
                      recip(stat[:, sc0 + 7:sc0 + 8], araw[:, 1052:1053], [araw], colsb)
                      ts("dve", stat[:, sc0 + 4:sc0 + 8], stat[:, sc0 + 4:sc0 + 8], neglam[:, 0:1], None, ALU.mult, None,
                         colsb + [neglam], colsb)
                      for qt in range(4):
                          c1 = rreg(0, qt)
                          c2 = rreg(1, qt)
                          ts("dve", ob[:, qt, :], araw[:, c1:c1 + 128], stat[:, sc0 + qt:sc0 + qt + 1], None, ALU.mult,
                             None, [araw] + colsb, [ob])
                          stt("dve", ob[:, qt, :], araw[:, c2:c2 + 128], stat[:, sc0 + 4 + qt:sc0 + 5 + qt], ob[:, qt, :],
                              ALU.mult, ALU.add, [araw, ob] + colsb, [ob])
                          act(junk[:, 0:128], ob[:, qt, :], AF.Square, [ob], [junk] + colsb,
                              accum_out=stat[:, sc0 + 8 + qt:sc0 + 9 + qt])
                      rstd_from(stat[:, sc0 + 8:sc0 + 12], stat[:, sc0 + 8:sc0 + 12], 128, colsb, colsb)
                      for qt in range(4):
                          stt("dve", mix[qt][:, h * 128:(h + 1) * 128], ob[:, qt, :], stat[:, sc0 + 8 + qt:sc0 + 9 + qt],
                              GS[:], ALU.mult, ALU.mult, [ob, GS] + colsb, [mix[qt]])

                  pend_fin.append(finalize)
                  if h == 3:
                      while pend_fin:
                          pend_fin.pop(0)()

              if stop_after == 'C':
                  break
              for _ in retg:
                  pass

              if stop_after == 'D':
                  break
              if sb == 0:
                  conv_cols()
              wo = [load_w(s_wout[:, hf * 512:(hf + 1) * 512].rearrange("(k p) n -> p k n", p=128), B_swout)
                    for hf in range(2)]
              mb = statb[61]
              for t in range(4):
                  for k in range(8):
                      OP("pe", lambda e, k=k, t=t: e.transpose(PT[:, k * 128:(k + 1) * 128],
                                                                mix[t][:, k * 128:(k + 1) * 128], ident[:]),
                         [mix[t], ident], [PT])
                  cp("act", mixT[:, :, t * 128:(t + 1) * 128], PT[:].rearrange("p (k q) -> p k q", k=8), [PT], [mixTb[t]])
                  xb = XS[xs_i[0] % 2]
                  xs_i[0] += 1
                  DMA("sp", xb[:], x[t0 + t * 128:t0 + (t + 1) * 128, :], w=[xb])
                  pbs = []
                  for hf in range(2):
                      pb = gbank()
                      for k in range(8):
                          mm(pb[:], mixT[:, k, t * 128:(t + 1) * 128], wo[hf][:, k, :], k == 0, k == 7,
                             [mixTb[t], wo[hf]], [pb])
                      act(junk[:, 0:512], pb[:], AF.Square, [pb], [junk, mb], accum_out=stat[:, 60 + hf:61 + hf])
                      pbs.append(pb)
                  tt("dve", stat[:, 62:63], stat[:, 60:61], stat[:, 61:62], ALU.add, [mb], [mb])
                  rstd_from(stat[:, 62:63], stat[:, 62:63], D, [mb], [mb])
                  for hf in range(2):
                      cs = slice(hf * 512, (hf + 1) * 512)
                      stt("dve", X1[t][:, cs], pbs[hf][:], stat[:, 62:63], G1[:, cs], ALU.mult, ALU.mult,
                          [pbs[hf], mb, G1], [X1[t]])
                      tt("dve", X1[t][:, cs], X1[t][:, cs], xb[:, cs], ALU.add, [X1[t], xb], [X1[t]])
                  if stop_after != 'E' and t >= 2:
                      norm_tile(t - 2, X1[t - 2][:], X1[t - 2], colB)
              if stop_after != 'E':
                  norm_tile(2, X1[2][:], X1[2], colB)
                  norm_tile(3, X1[3][:], X1[3], colB)

              if dbg and "x1" in dbg and sb == 0:
                  for t in range(4):
                      DMA("sp", dbg_out["x1"][t * 128:(t + 1) * 128, :], X1[t][:], r=[X1[t]])

              if stop_after == 'E':
                  break
              wd_i = [0]
              u_i = [0]
              pend_mul = []
              for g in range(11):
                  wsb = load_w(s_wup[:, g, :].rearrange("(k p) n -> p k n", p=128), B_swup[min(g // 3, 3)])
                  for j in range(2):
                      cidx = g * 2 + j
                      ug, uv = Ug[u_i[0] % 2], Uv[u_i[0] % 2]
                      yg, yv = Yg[u_i[0] % 2], Yv[u_i[0] % 2]
                      u_i[0] += 1
                      for (ut, yt, coff, cc) in ((ug, yg, j * 128, cidx), (uv, yv, 256 + j * 128, NFC + cidx)):
                          pb = fbank()
                          for k in range(8):
                              mm(pb[:], wsb[:, k, coff:coff + 128], hT[:, k, :], k == 0, k == 7, [wsb, hT], [pb])
                          hp_, hn_ = sb % 2, (sb + 1) % 2
                          uh = UH[id(ut)]
                          cp("pool", ut[:, 0:2], HALO[:, hp_, cc, :], [HALOb[hp_][cc]], [uh])
                          cp("act", ut[:, 2:514], pb[:], [pb], [ut])
                          act(yt[:], pb[:], AF.Identity, [pb, convc], [yt], scale=convc[:, 2, cc:cc + 1],
                              bias=convc[:, 3, cc:cc + 1])
                          cp("pool", HALO[:, hn_, cc, :], ut[:, 512:514], [ut], [HALOb[hn_][cc]])
                          stt("dve", yt[:], ut[:, 1:513], convc[:, 1, cc:cc + 1], yt[:], ALU.mult, ALU.add,
                              [ut, uh, convc, yt], [yt])
                          stt("dve", yt[:], ut[:, 0:512], convc[:, 0, cc:cc + 1], yt[:], ALU.mult, ALU.add,
                              [ut, uh, convc, yt], [yt])
                      if pend_mul:
                          pend_mul.pop()()
                      act(yg[:], yg[:], AF.Silu, [yg], [yg])
                      pend_mul.append(lambda yg=yg, yv=yv, cidx=cidx: tt("pool", actT[:, cidx, :], yg[:], yv[:], ALU.mult,
                                                                         [yg, yv], [actTb[cidx]]))

              pend_mul.pop()()
              fb = statb[62]
              for qq in range(4):
                  wdb = WD[wd_i[0] % 2]
                  wd_i[0] += 1
                  DMA("sp", wdb[:], s_wdn[:, qq * 256:(qq + 1) * 256].rearrange("(c p) n -> p c n", p=128),
                      r=[B_swdn[qq // 2]], w=[wdb])
                  for t in range(4):
                      pb = gbank()
                      for c_ in range(NFC):
                          mm(pb[:, 0:256], actT[:, c_, t * 128:(t + 1) * 128], wdb[:, c_, :], c_ == 0, c_ == NFC - 1,
                             [actTb[c_], wdb], [pb])
                      cp("dve", fsb[t][:, qq * 256:(qq + 1) * 256], pb[:, 0:256], [pb], [fsb[t]])
                  if sb + 1 < NSB:
                      phaseA_tile(sb + 1, qq)
              if sb + 1 < NSB:
                  rezero()
              if sb + 1 < NSB:
                  for g_ in range(2):
                      pref_w[g_] = load_w(s_win[:, g_ * 512:(g_ + 1) * 512].rearrange("(k p) n -> p k n", p=128),
                                          B_swin[g_])

              def epi_tile(t, t0=t0, fb=fb):
                  act(junk[:], fsb[t][:], AF.Square, [fsb[t]], [junk, fb], accum_out=stat[:, 63:64])
                  rstd_from(stat[:, 63:64], stat[:, 63:64], D, [fb], [fb])
                  stt("dve", fsb[t][:], fsb[t][:], stat[:, 63:64], G2[:], ALU.mult, ALU.mult, [fsb[t], fb, G2], [fsb[t]])
                  tt("dve", X1[t][:], X1[t][:], fsb[t][:], ALU.add, [X1[t], fsb[t]], [X1[t]])
                  DMA("pool", y[t0 + t * 128:t0 + (t + 1) * 128, :], X1[t][:], r=[X1[t]])

              for t in range(4):
                  pend_epi.append(lambda t=t, f=epi_tile: f(t))
              if sb + 1 == NSB:
                  while pend_epi:
                      pend_epi.pop(0)()


        try:
            _main()
        except _Stop:
            pass
        S.final_wait("sp", [t.b for t in X1])
        S.emit()
    return nc


_CACHE = {}


def _inputs_for_core(b, inp):
    f = lambda a: np.ascontiguousarray(np.asarray(a, dtype=np.float32))
    rowpack = np.concatenate([
        np.asarray(inp["c"])[b].reshape(-1),
        np.asarray(inp["g_pre_mix"]).reshape(-1),
        np.asarray(inp["g_post_mix"]).reshape(-1),
        np.asarray(inp["g_pre_ffn"]).reshape(-1),
        np.asarray(inp["g_post_ffn"]).reshape(-1),
        np.asarray(inp["lam_q1"]).reshape(-1), np.asarray(inp["lam_k1"]).reshape(-1),
        np.asarray(inp["lam_q2"]).reshape(-1), np.asarray(inp["lam_k2"]).reshape(-1),
        np.asarray(inp["g_da_subln"]).reshape(-1),
    ]).reshape(1, -1)
    return {
        "x": f(np.asarray(inp["x"])[b]),
        "rowpack": f(rowpack),
        "w_ada": f(np.asarray(inp["w_ada"])[0]),
        "b_ada": f(np.asarray(inp["b_ada"])[0].reshape(1, -1)),
        "w_in": f(np.asarray(inp["w_in"])[0]),
        "w_out": f(np.asarray(inp["w_out"])[0]),
        "w_up": f(np.asarray(inp["w_up"])[0]),
        "conv_w": f(np.asarray(inp["conv_w"])[0]),
        "conv_b": f(np.asarray(inp["conv_b"])[0].reshape(1, -1)),
        "w_down": f(np.asarray(inp["w_down"])[0]),
    }


def kernel(**inputs):
    if "nc" not in _CACHE:
        _CACHE["nc"] = build()
    nc = _CACHE["nc"]
    in_maps = [_inputs_for_core(b, inputs) for b in range(8)]
    res = run_bass_kernel_spmd(nc, in_maps, core_ids=list(range(8)))
    out = np.stack([np.asarray(r["y"]) for r in res.results], axis=0)
    return out.astype(np.float32)
```
